# Optimizing a Trainium2 kernel written in Bass

```python
import math
import jax, jax.numpy as jnp
from jax import lax
import numpy as np

D_MODEL = 2048
BATCH = 8
SEQ = 2048
DEPTH = 2
DEC_BATCH = 4
DEC_SEQ = 4096
PAST_LEN = 128

HEAD_DIM = 64
RET_HEADS = 8
RET_W = RET_HEADS * HEAD_DIM
RET_CHUNK = 128
NA_HEADS = 12
NA_W = NA_HEADS * HEAD_DIM
GRID_W = 64
NA_KH_MAX = 8
NA_KW = 16
NA_QB = 16
NA_HALO = NA_QB + NA_KW
DIFF_HEADS = 6
DIFF_QK_DIM = HEAD_DIM
DIFF_V_DIM = 2 * HEAD_DIM
DIFF_W = DIFF_HEADS * DIFF_V_DIM
DIFF_QBLOCK = 128
D_MIX = RET_W + NA_W + DIFF_W
D_IN = 4 * RET_W + 3 * NA_W + 3 * DIFF_W
D_FF = 5632
CONV_W = 3
NORM_EPS = 1e-6
NEG_INF = -1e30

kernel_name = 'hybrid_bidir_encoder'


def rms_norm(x, w):
    xf = x.astype(jnp.float32)
    y = xf * lax.rsqrt(jnp.mean(xf * xf, axis=-1, keepdims=True) + NORM_EPS)
    return (y * w.astype(jnp.float32)).astype(x.dtype)


def head_norm(x, w, center):
    xf = x.astype(jnp.float32)
    if center:
        xf = xf - jnp.mean(xf, axis=-1, keepdims=True)
    y = xf * lax.rsqrt(jnp.mean(xf * xf, axis=-1, keepdims=True) + NORM_EPS)
    return y.reshape(x.shape[:-2] + (-1,)) * w.astype(jnp.float32)


def retention_scan(q, k, v, log_gamma, include_diag):
    B, S, H, d = q.shape
    n = S // RET_CHUNK

    def chunks(t):
        return t.reshape(B, n, RET_CHUNK, H, d).transpose(1, 0, 3, 2, 4)

    pos = jnp.arange(RET_CHUNK, dtype=jnp.float32)
    delta = pos[:, None] - pos[None, :]
    lower = (delta >= 0) if include_diag else (delta > 0)
    lg = log_gamma[:, None, None]
    decay_in = jnp.where(lower[None], jnp.exp(lg * jnp.maximum(delta, 0.0)[None]), 0.0)
    q_decay = jnp.exp(log_gamma[:, None] * (pos + 1.0)[None])[..., None]
    k_decay = jnp.exp(log_gamma[:, None] * (RET_CHUNK - 1.0 - pos)[None])[..., None]
    chunk_decay = jnp.exp(log_gamma * RET_CHUNK)[:, None, None]

    def step(state, qkv):
        qc, kc, vc = qkv
        inner = jnp.einsum('bhid,bhjd->bhij', qc, kc) * decay_in
        out = (jnp.einsum('bhij,bhje->bhie', inner, vc)
               + jnp.einsum('bhid,bhde->bhie', qc * q_decay, state))
        state = state * chunk_decay + jnp.einsum('bhjd,bhje->bhde', kc * k_decay, vc)
        return state, out

    state0 = jnp.zeros((B, H, d, d), jnp.float32)
    _, out = lax.scan(step, state0, (chunks(q), chunks(k), chunks(v)))
    return out.transpose(1, 0, 3, 2, 4).reshape(B, S, H, d)


def retention_mixer(q, k, v, g, decay_fwd, decay_bwd, norm_w):
    B, S = q.shape[:2]
    qf = q.reshape(B, S, RET_HEADS, HEAD_DIM).astype(jnp.float32)
    kf = k.reshape(B, S, RET_HEADS, HEAD_DIM).astype(jnp.float32) * (HEAD_DIM ** -0.5)
    vf = v.reshape(B, S, RET_HEADS, HEAD_DIM).astype(jnp.float32)
    lg_f = jax.nn.log_sigmoid(decay_fwd.astype(jnp.float32))
    lg_b = jax.nn.log_sigmoid(decay_bwd.astype(jnp.float32))
    fwd = retention_scan(qf, kf, vf, lg_f, True)
    bwd = jnp.flip(retention_scan(jnp.flip(qf, 1), jnp.flip(kf, 1), jnp.flip(vf, 1), lg_b, False), 1)
    y = head_norm(fwd + bwd, norm_w, True)
    return (jax.nn.silu(g.astype(jnp.float32)) * y).astype(g.dtype)


def neighbourhood_attention(q, k, v, rpb):
    B, S = q.shape[:2]
    rows = S // GRID_W
    kh = min(NA_KH_MAX, rows)
    ncb = GRID_W // NA_QB
    qg = q.reshape(B, rows, ncb, NA_QB, NA_HEADS, HEAD_DIM)
    q_rows = jnp.moveaxis(qg, 1, 0)
    kg = k.reshape(B, rows, GRID_W, NA_HEADS, HEAD_DIM)
    vg = v.reshape(B, rows, GRID_W, NA_HEADS, HEAD_DIM)
    blk = np.arange(ncb)
    cb = np.clip(blk * NA_QB - NA_KW // 2, 0, GRID_W - NA_HALO)
    col_idx = cb[:, None] + np.arange(NA_HALO)[None, :]
    qcol = blk[:, None] * NA_QB + np.arange(NA_QB)[None, :]
    cs = np.clip(qcol - NA_KW // 2, 0, GRID_W - NA_KW)
    kcol = col_idx[:, None, :]
    in_win = (kcol >= cs[..., None]) & (kcol < cs[..., None] + NA_KW)
    dc_idx = np.clip(kcol - qcol[..., None] + NA_KW - 1, 0, 2 * NA_KW - 2)
    rpb_cols = rpb.astype(jnp.float32)[:, :, dc_idx]
    mask = jnp.asarray(in_win)[None, None, :, :, None, :]
    scale = HEAD_DIM ** -0.5

    def row(args):
        q_row, r = args
        rs = jnp.clip(r - kh // 2, 0, rows - kh)
        k_blk = lax.dynamic_slice_in_dim(kg, rs, kh, axis=1)[:, :, col_idx]
        v_blk = lax.dynamic_slice_in_dim(vg, rs, kh, axis=1)[:, :, col_idx]
        s = jnp.einsum('bnqhd,bknjhd->bhnqkj', q_row, k_blk).astype(jnp.float32) * scale
        dr_idx = rs + jnp.arange(kh) - r + NA_KH_MAX - 1
        bias = rpb_cols[:, dr_idx].transpose(0, 2, 3, 1, 4)
        s = jnp.where(mask, s + bias[None], NEG_INF)
        p = jax.nn.softmax(s.reshape(B, NA_HEADS, ncb, NA_QB, kh * NA_HALO), axis=-1)
        p = p.reshape(B, NA_HEADS, ncb, NA_QB, kh, NA_HALO).astype(v.dtype)
        return jnp.einsum('bhnqkj,bknjhd->bnqhd', p, v_blk)

    out = lax.map(row, (q_rows, jnp.arange(rows)))
    return jnp.moveaxis(out, 0, 1).reshape(B, S, NA_W)


def diff_attention(q, k, v, lam, lam_init, norm_w):
    B, S = q.shape[:2]
    qh = q.reshape(B, S, DIFF_HEADS, 2, DIFF_QK_DIM)
    kh = k.reshape(B, S, DIFF_HEADS, 2, DIFF_QK_DIM)
    vh = v.reshape(B, S, DIFF_HEADS, DIFF_V_DIM)
    scale = DIFF_QK_DIM ** -0.5
    slopes = 2.0 ** (-8.0 * (jnp.arange(DIFF_HEADS, dtype=jnp.float32) + 1.0) / DIFF_HEADS)
    nb = S // DIFF_QBLOCK
    q_blocks = qh.reshape(B, nb, DIFF_QBLOCK, DIFF_HEADS, 2, DIFF_QK_DIM).transpose(1, 0, 2, 3, 4, 5)
    kpos = jnp.arange(S, dtype=jnp.float32)

    def block(args):
        q_blk, start = args
        qpos = start.astype(jnp.float32) + jnp.arange(DIFF_QBLOCK, dtype=jnp.float32)
        bias = -slopes[:, None, None] * jnp.abs(qpos[:, None] - kpos[None, :])
        s = jnp.einsum('bqhcd,bkhcd->bhcqk', q_blk, kh).astype(jnp.float32) * scale + bias[None, :, None]
        p = jax.nn.softmax(s, axis=-1)
        a = p[:, :, 0] - lam * p[:, :, 1]
        return jnp.einsum('bhqk,bkhe->bqhe', a.astype(v.dtype), vh)

    out = lax.map(block, (q_blocks, jnp.arange(nb) * DIFF_QBLOCK))
    out = out.transpose(1, 0, 2, 3, 4).reshape(B, S, DIFF_HEADS, DIFF_V_DIM)
    return (head_norm(out, norm_w, False) * (1.0 - lam_init)).astype(v.dtype)


def conv_ffn(h, w_up, conv_w, conv_b, w_down):
    u = h @ w_up
    a, g = jnp.split(u, 2, axis=-1)
    gp = jnp.pad(g, ((0, 0), (1, 1), (0, 0)))
    g = gp[:, :-2] * conv_w[0] + gp[:, 1:-1] * conv_w[1] + gp[:, 2:] * conv_w[2] + conv_b
    return (jax.nn.gelu(g) * a) @ w_down


def encoder_trunk(x, norm1_w, w_in, ret_decay_fwd, ret_decay_bwd, ret_norm_w, na_rpb,
                  diff_lambda_q1, diff_lambda_k1, diff_lambda_q2, diff_lambda_k2, diff_norm_w,
                  w_out, norm2_w, ffn_w_up, ffn_conv_w, ffn_conv_b, ffn_w_down, final_norm_w):
    splits = list(np.cumsum([RET_W] * 4 + [NA_W] * 3 + [DIFF_W] * 3)[:-1])
    for l in range(DEPTH):
        h = rms_norm(x, norm1_w[l])
        proj = h @ w_in[l]
        rq, rk, rv, rg, nq, nk, nv, dq, dk, dv = jnp.split(proj, splits, axis=-1)
        ret_out = retention_mixer(rq, rk, rv, rg, ret_decay_fwd[l], ret_decay_bwd[l], ret_norm_w[l])
        na_out = neighbourhood_attention(nq, nk, nv, na_rpb[l])
        lam_init = 0.8 - 0.6 * math.exp(-0.3 * l)
        lam = (jnp.exp(jnp.sum(diff_lambda_q1[l].astype(jnp.float32) * diff_lambda_k1[l].astype(jnp.float32)))
               - jnp.exp(jnp.sum(diff_lambda_q2[l].astype(jnp.float32) * diff_lambda_k2[l].astype(jnp.float32)))
               + lam_init)
        diff_out = diff_attention(dq, dk, dv, lam, lam_init, diff_norm_w[l])
        mix = jnp.concatenate([ret_out, na_out.astype(x.dtype), diff_out], axis=-1)
        x = x + mix @ w_out[l]
        h = rms_norm(x, norm2_w[l])
        x = x + conv_ffn(h, ffn_w_up[l], ffn_conv_w[l], ffn_conv_b[l], ffn_w_down[l])
    return rms_norm(x, final_norm_w)


def setup_inputs(seed: int = 0) -> dict:
    key = jax.random.key(seed)
    ks = jax.random.split(key, 20)
    f32 = jnp.float32

    def nrm(k, shape, scale):
        return jax.random.normal(k, shape, f32) * scale

    def gain(k, shape):
        return 1.0 + nrm(k, shape, 0.01)

    decay_init = jnp.asarray(np.log(2.0 ** (5 + np.arange(RET_HEADS)) - 1.0), f32)
    return {
        'x_prompt': nrm(ks[0], (BATCH, SEQ, D_MODEL), 1.0),
        'x_sample': nrm(ks[1], (DEC_BATCH, DEC_SEQ, D_MODEL), 1.0),
        'norm1_w': gain(ks[2], (DEPTH, D_MODEL)),
        'w_in': nrm(ks[3], (DEPTH, D_MODEL, D_IN), D_MODEL ** -0.5),
        'ret_decay_fwd': decay_init + nrm(ks[4], (DEPTH, RET_HEADS), 0.01),
        'ret_decay_bwd': decay_init + nrm(ks[5], (DEPTH, RET_HEADS), 0.01),
        'ret_norm_w': gain(ks[6], (DEPTH, RET_W)),
        'na_rpb': nrm(ks[7], (DEPTH, NA_HEADS, 2 * NA_KH_MAX - 1, 2 * NA_KW - 1), 0.02),
        'diff_lambda_q1': nrm(ks[8], (DEPTH, DIFF_QK_DIM), 0.1),
        'diff_lambda_k1': nrm(ks[9], (DEPTH, DIFF_QK_DIM), 0.1),
        'diff_lambda_q2': nrm(ks[10], (DEPTH, DIFF_QK_DIM), 0.1),
        'diff_lambda_k2': nrm(ks[11], (DEPTH, DIFF_QK_DIM), 0.1),
        'diff_norm_w': gain(ks[12], (DEPTH, DIFF_W)),
        'w_out': nrm(ks[13], (DEPTH, D_MIX, D_MODEL), D_MIX ** -0.5),
        'norm2_w': gain(ks[14], (DEPTH, D_MODEL)),
        'ffn_w_up': nrm(ks[15], (DEPTH, D_MODEL, 2 * D_FF), D_MODEL ** -0.5),
        'ffn_conv_w': nrm(ks[16], (DEPTH, CONV_W, D_FF), CONV_W ** -0.5),
        'ffn_conv_b': nrm(ks[17], (DEPTH, D_FF), 0.01),
        'ffn_w_down': nrm(ks[18], (DEPTH, D_FF, D_MODEL), D_FF ** -0.5),
        'final_norm_w': gain(ks[19], (D_MODEL,)),
    }


def reference(x_prompt, x_sample, norm1_w, w_in, ret_decay_fwd, ret_decay_bwd, ret_norm_w, na_rpb,
              diff_lambda_q1, diff_lambda_k1, diff_lambda_q2, diff_lambda_k2, diff_norm_w,
              w_out, norm2_w, ffn_w_up, ffn_conv_w, ffn_conv_b, ffn_w_down, final_norm_w):
    y_prompt = encoder_trunk(x_prompt, norm1_w, w_in, ret_decay_fwd, ret_decay_bwd, ret_norm_w, na_rpb,
                             diff_lambda_q1, diff_lambda_k1, diff_lambda_q2, diff_lambda_k2, diff_norm_w,
                             w_out, norm2_w, ffn_w_up, ffn_conv_w, ffn_conv_b, ffn_w_down, final_norm_w)
    y_sample = encoder_trunk(x_sample, norm1_w, w_in, ret_decay_fwd, ret_decay_bwd, ret_norm_w, na_rpb,
                             diff_lambda_q1, diff_lambda_k1, diff_lambda_q2, diff_lambda_k2, diff_norm_w,
                             w_out, norm2_w, ffn_w_up, ffn_conv_w, ffn_conv_b, ffn_w_down, final_norm_w)
    return (y_prompt, y_sample)
```

```python
import math
import os
import contextlib
import numpy as np
import concourse.bass as bass
import concourse.mybir as mybir
from concourse.bass_utils import run_bass_kernel_spmd

F32 = mybir.dt.float32
BF16 = mybir.dt.bfloat16
AF = mybir.ActivationFunctionType
ALU = mybir.AluOpType
AX = mybir.AxisListType

ENGS = ("pe", "dve", "act", "pool", "sp")
T = 4096
D = 2048
NEG = -30000.0
EPS = 1e-6


class SemSlot:
    __slots__ = ("count", "handle")

    def __init__(self):
        self.count = 0
        self.handle = None


class Buf:
    __slots__ = ("name", "writers", "readers", "slot")

    def __init__(self, name="b"):
        self.name = name
        self.writers = []
        self.readers = []
        self.slot = None


class Op:
    __slots__ = ("eng", "fn", "deps", "is_dma", "slot", "target", "needed")

    def __init__(self, eng, fn, is_dma):
        self.eng = eng
        self.fn = fn
        self.deps = []
        self.is_dma = is_dma
        self.slot = None
        self.target = None
        self.needed = False


class Prog:
    def __init__(self, nc):
        self.nc = nc
        self.ops = {e: [] for e in ENGS}
        self.slots = []
        self.free_slots = []
        self.used_slots = []
        self.bar = {e: [] for e in ENGS}
        self.dma_since_bar = []

    def _dep(self, op, prod):
        if prod is op:
            return
        if prod.eng == "pe" and op.eng == "pe" and not prod.is_dma and not op.is_dma:
            return
        op.deps.append(prod)

    def op(self, eng, fn, reads=(), writes=(), dma_key=None):
        is_dma = dma_key is not None
        o = Op(eng, fn, is_dma)
        if self.bar[eng]:
            o.deps.extend(self.bar[eng])
            self.bar[eng] = []
        for b in reads:
            for w in b.writers:
                self._dep(o, w)
        for b in writes:
            if is_dma and b.writers and not b.readers and all(w.is_dma for w in b.writers):
                b.writers.append(o)
                continue
            for r in b.readers:
                self._dep(o, r)
            for w in b.writers:
                self._dep(o, w)
            b.writers = [o]
            b.readers = []
        for b in reads:
            if o not in b.writers:
                b.readers.append(o)
        if is_dma:
            if dma_key.slot is None:
                if self.free_slots:
                    dma_key.slot = self.free_slots.pop()
                else:
                    dma_key.slot = SemSlot()
                    self.slots.append(dma_key.slot)
                self.used_slots.append((dma_key, dma_key.slot))
            s = dma_key.slot
            s.count += 16
            o.slot = s
            o.target = s.count
            self.dma_since_bar.append(o)
        self.ops[eng].append(o)
        return o

    def barrier(self):
        deps = list(self.dma_since_bar)
        for e in ENGS:
            for o in reversed(self.ops[e]):
                if not o.is_dma:
                    deps.append(o)
                    break
        for e in ENGS:
            self.bar[e] = list(deps)
        self.dma_since_bar = []
        for b, s in self.used_slots:
            b.slot = None
            self.free_slots.append(s)
        self.used_slots = []

    def dma(self, q, out, in_, reads, writes, key):
        return self.op(q, lambda e: e.dma_start(out=out, in_=in_), reads, writes, dma_key=key)

    def emit(self, final_ops=()):
        nc = self.nc
        for e in ENGS:
            for o in self.ops[e]:
                for d in o.deps:
                    if not d.is_dma:
                        d.needed = True
        cnt = {e: 0 for e in ENGS}
        for e in ENGS:
            for o in self.ops[e]:
                if not o.is_dma and o.needed:
                    cnt[e] += 1
                    o.target = cnt[e]
        with contextlib.ExitStack() as st:
            esem = {e: st.enter_context(nc.semaphore("s_" + e)) for e in ENGS}
            for i, s in enumerate(self.slots):
                s.handle = st.enter_context(nc.semaphore("d%d" % i))
            block = st.enter_context(nc.Block())
            prog = self

            def run(eng_name, eng):
                waited = {}
                for o in prog.ops[eng_name]:
                    need = {}
                    for d in o.deps:
                        s = d.slot.handle if d.is_dma else esem[d.eng]
                        k = id(s)
                        if waited.get(k, 0) >= d.target:
                            continue
                        if k not in need or need[k][1] < d.target:
                            need[k] = (s, d.target)
                    for k, (s, v) in need.items():
                        eng.wait_ge(s, v)
                        waited[k] = v
                    ins = o.fn(eng)
                    if o.is_dma:
                        ins.then_inc(o.slot.handle, 16)
                    elif o.needed:
                        ins.then_inc(esem[eng_name], 1)
                if eng_name == "sp":
                    fin = {}
                    for o in final_ops:
                        k = id(o.slot)
                        if k not in fin or fin[k][1] < o.target:
                            fin[k] = (o.slot.handle, o.target)
                    for k, (s, v) in fin.items():
                        eng.wait_ge(s, v)
                    for sl_ in prog.slots:
                        if sl_.count > 0:
                            eng.wait_ge(sl_.handle, sl_.count)

            @block.tensor
            def _(e):
                run("pe", e)

            @block.vector
            def _(e):
                run("dve", e)

            @block.scalar
            def _(e):
                run("act", e)

            @block.gpsimd
            def _(e):
                run("pool", e)

            @block.sync
            def _(e):
                run("sp", e)


class Arena:
    def __init__(self, tensor, nbytes):
        self.t = tensor
        self.cap = nbytes
        self.off = 0
        self.base = 0

    def mark(self):
        self.base = self.off

    def reset(self):
        if os.environ.get("ARENA_DBG"):
            print("arena high-water", getattr(self, "hw", 0))
        self.hw = 0
        self.off = self.base

    def alloc(self, shape, dt):
        n = 1
        for s in shape:
            n *= s
        nb = n * (4 if dt == F32 else 2)
        off = (self.off + 63) // 64 * 64
        self.off = off + nb
        assert self.off <= self.cap, ("arena overflow", self.off, self.cap)
        self.hw = max(getattr(self, "hw", 0), self.off)
        v = self.t[:, off // 2:(off + nb) // 2]
        if dt == F32:
            v = v.bitcast(F32)
        if len(shape) == 2:
            v = v.rearrange("p (a b) -> p a b", b=shape[1])
        elif len(shape) == 3:
            v = v.rearrange("p (a b c) -> p a b c", b=shape[1], c=shape[2])
        return v


class Ring:
    def __init__(self, arena, n, shape, dt, name):
        self.items = [(arena.alloc(shape, dt), Buf("%s%d" % (name, i))) for i in range(n)]
        self.i = 0

    def next(self):
        it = self.items[self.i % len(self.items)]
        self.i += 1
        return it


C_ID = 0
C_NIDXF = 128
C_NIDXB = 256
C_NPOS1 = 384
C_NPOSB = 512
C_NK1 = 640
C_NK0 = 641
C_ONE = 642
C_KAUG = 643
C_IOTA = 644
C_NEAR = 1156
C_BCA = 3204
C_BCB = 3396
CW = 3588

SLOPES = [2.0 ** (-8.0 * (h + 1.0) / 6.0) for h in range(6)]


def make_consts():
    c = np.zeros((128, CW), np.float32)
    p = np.arange(128)[:, None].astype(np.float64)
    i = np.arange(128)[None, :].astype(np.float64)
    c[:, C_ID:C_ID + 128] = np.eye(128)
    c[:, C_NIDXF:C_NIDXF + 128] = np.where(i >= p, -(i - p), -1e6)
    c[:, C_NIDXB:C_NIDXB + 128] = np.where(p > i, -(p - i), -1e6)
    c[:, C_NPOS1:C_NPOS1 + 128] = -(i + 1)
    c[:, C_NPOSB:C_NPOSB + 128] = -(128 - i)
    c[:, C_NK1] = -(127 - p[:, 0])
    c[:, C_NK0] = -p[:, 0]
    c[:, C_ONE] = 1.0
    c[64:66, C_KAUG] = 1.0
    c[66:68, C_KAUG] = -2.0
    ii = np.arange(512)[None, :].astype(np.float64)
    c[:, C_IOTA:C_IOTA + 512] = ii
    for m in range(4):
        c[:, C_NEAR + 512 * m:C_NEAR + 512 * (m + 1)] = -np.abs(ii - 128 * m - p)
    for h in range(6):
        for d in range(32):
            c[:, C_BCA + h * 32 + d] = -SLOPES[h] * (128 * d + p[:, 0])
            c[:, C_BCB + h * 32 + d] = -SLOPES[h] * (128 * d - p[:, 0])
    return c


def na_rs(kind, rq):
    if kind == 0:
        s, r = divmod(rq, 32)
        return 32 * s + min(max(r - 4, 0), 24)
    return min(max(rq - 4, 0), 56)


def na_plan():
    plan = []
    slot = 0
    for rq in range(64):
        a, b = na_rs(0, rq), na_rs(1, rq)
        if a == b:
            plan.append((a, 4, None))
        else:
            lo = min(a, b)
            hi = max(a, b) + 8
            nk = (hi - lo + 1) // 2
            plan.append((lo, nk, slot))
            slot += nk
    return plan, slot


NA_PLAN, NA_NSLOT = na_plan()
FLG_W = 2 + NA_NSLOT


def make_flags(kind):
    f = np.zeros((128, FLG_W), np.float32)
    f[:, 0] = 1.0 if kind == 1 else 0.0
    f[:, 1] = 0.0 if kind == 1 else NEG
    for rq, (ws, nk, slot) in enumerate(NA_PLAN):
        if slot is None:
            continue
        rs = na_rs(kind, rq)
        for m in range(nk):
            for half in range(2):
                rk = ws + 2 * m + half
                ok = rs <= rk < rs + 8
                f[64 * half:64 * half + 64, 2 + slot + m] = 0.0 if ok else NEG
    return f


def make_tt(rpb):
    ck = np.arange(64)[:, None]
    cq = np.arange(64)[None, :]
    cs = np.clip(cq - 8, 0, 48)
    inwin = (ck >= cs) & (ck < cs + 16)
    dc = np.clip(ck - cq + 15, 0, 30)
    out = np.full((2, 12, 128, 15, 64), NEG, np.float32)
    for half in range(2):
        for ap in range(15):
            a = ap + half
            if a > 14:
                continue
            g = rpb[:, :, a, :][:, :, dc]
            out[:, :, 64 * half:64 * half + 64, ap, :] = np.where(inwin[None, None], g, np.float32(NEG))
    out = out.reshape(2, 6, 2, 128, 15, 64).transpose(0, 1, 3, 2, 4, 5)
    return np.ascontiguousarray(out.reshape(2, 6, 128, 2 * 15 * 64))


FM_GROUPS = [(0, 4, 0, 1.0), (512, 4, 512, 0.125), (2048, 6, 1024, 0.125), (2816, 6, 1792, 1.0),
             (4352, 6, 2560, 0.125), (5120, 6, 3328, 1.0)]
TM_SLABS = [(512, 0, True), (1024, 512, False), (1536, 1024, False), (3584, 1536, False),
            (4096, 2048, False), (6144, 2560, False)]


def build(stages=("s0", "A", "B1", "B2", "B3", "C", "D", "F"), nlayers=2, dbg=()):
    nc = bass.Bass("TRN2", target_bir_lowering=False)

    def din(name, shape, dt=F32):
        return nc.dram_tensor(name, shape, dt, kind="ExternalInput").ap()

    x_in = din("x", [T, D])
    w_in = din("w_in", [2, D, 6656])
    w_out = din("w_out", [2, D, D])
    w_up = din("w_up", [2, D, 11264])
    w_dn = din("w_dn", [2, 5632, D])
    nrm = din("nrm", [128, 5, 16])
    cwb = din("cwb", [2, 128, 44, 4])
    rdec = din("rdec", [2, 128, 24])
    rnw = din("rnw", [2, 128, 512])
    dnw = din("dnw", [2, 128, 768])
    dlam = din("dlam", [2, 128, 256])
    tt_in = din("tt", [2, 6, 128, 1920])
    cst = din("cst", [128, CW])
    augq = din("augq", [6, 4, 512])
    flg = din("flg", [128, FLG_W])
    y_out = nc.dram_tensor("y", [T, D], F32, kind="ExternalOutput").ap()

    def dscr(name, shape, dt):
        if name in dbg:
            return nc.dram_tensor(name, shape, dt, kind="ExternalOutput").ap()
        return nc.dram_tensor(name, shape, dt).ap()

    xT = dscr("xT", [D, T], F32)
    qkT = dscr("qkT", [4096, T], BF16)
    tokm = dscr("tokm", [T, 3072], BF16)
    mixT = dscr("mixT", [D, T], BF16)
    h2T = dscr("h2T", [D, T + 64], BF16)
    rs2 = dscr("rs2", [128, T + 32], F32)

    P = Prog(nc)
    final_ops = []
    with contextlib.ExitStack() as st:
        ARENA_BYTES = 186 * 1024
        arena_t = st.enter_context(nc.sbuf_tensor("arena", [128, ARENA_BYTES // 2], BF16))
        A = Arena(arena_t, ARENA_BYTES)
        psum_all = st.enter_context(nc.psum_tensor("psum_all", [128, 4096], F32))
        banks = [psum_all[:, i * 512:(i + 1) * 512] for i in range(8)]

        def PB(name="ps"):
            return [Buf("%s%d" % (name, i)) for i in range(8)]

        def mm(out, lhsT, rhs, start, stop, reads, writes):
            return P.op("pe", lambda e: e.matmul(out, lhsT=lhsT, rhs=rhs, start=start, stop=stop), reads, writes)

        def tr(out, in_, ident, reads, writes):
            return P.op("pe", lambda e: e.transpose(out, in_, ident), reads, writes)

        def act(out, in_, func, reads, writes, bias=None, scale=None, accum=None, eng="act"):
            kw = {}
            if bias is not None:
                kw["bias"] = bias
            if scale is not None:
                kw["scale"] = scale
            if accum is not None:
                kw["accum_out"] = accum
            return P.op(eng, lambda e: e.activation(out=out, in_=in_, func=func, **kw), reads, writes)

        def amul(out, in_, m, reads, writes):
            return P.op("act", lambda e: e.mul(out, in_, m), reads, writes)

        def tt(eng, out, in0, in1, op, reads, writes):
            return P.op(eng, lambda e: e.tensor_tensor(out=out, in0=in0, in1=in1, op=op), reads, writes)

        def ts(eng, out, in0, s1, s2, op0, op1, reads, writes):
            if op1 is None:
                return P.op(eng, lambda e: e.tensor_scalar(out=out, in0=in0, scalar1=s1, scalar2=None, op0=op0), reads, writes)
            return P.op(eng, lambda e: e.tensor_scalar(out=out, in0=in0, scalar1=s1, scalar2=s2, op0=op0, op1=op1), reads, writes)

        def stt(eng, out, in0, scalar, in1, op0, op1, reads, writes):
            return P.op(eng, lambda e: e.scalar_tensor_tensor(out=out, in0=in0, scalar=scalar, in1=in1, op0=op0, op1=op1), reads, writes)

        def cp(eng, out, in_, reads, writes):
            if eng == "act":
                return P.op("act", lambda e: e.copy(out, in_), reads, writes)
            return P.op(eng, lambda e: e.tensor_copy(out, in_), reads, writes)

        def rsum(eng, out, in_, reads, writes):
            return P.op(eng, lambda e: e.reduce_sum(out=out, in_=in_, axis=AX.X), reads, writes)

        def recip(out, in_, reads, writes):
            return P.op("dve", lambda e: e.reciprocal(out=out, in_=in_), reads, writes)

        def mset(eng, ap, val, writes):
            return P.op(eng, lambda e: e.memset(ap, val), [], writes)

        def load_cast(stg_r, q, cast_eng, dst, dbuf, src):
            a_, b_ = src.shape[1], src.shape[2]
            stg, bst = stg_r.next()
            view = stg[:, 0:a_ * b_].rearrange("p (a b) -> p a b", b=b_)
            P.dma(q, view, src, [], [bst], bst)
            cp(cast_eng, dst, view, [bst], [dbuf])

        cst_sb = A.alloc([C_IOTA], F32)
        b_cst = Buf("cst")
        P.dma("sp", cst_sb, cst[:, 0:C_IOTA], [], [b_cst], b_cst)
        flg_sb = A.alloc([FLG_W], F32)
        b_flg = Buf("flg")
        P.dma("sp", flg_sb, flg, [], [b_flg], b_flg)
        nrm_sb = A.alloc([5, 16], F32)
        b_nrm = Buf("nrm")
        P.dma("sp", nrm_sb, nrm, [], [b_nrm], b_nrm)
        ident = cst_sb[:, C_ID:C_ID + 128]
        ident_bf = A.alloc([128], BF16)
        b_idb = Buf("identbf")
        cp("dve", ident_bf, ident, [b_cst], [b_idb])
        keep_col = flg_sb[:, 0:1]
        xneg_col = flg_sb[:, 1:2]
        A.mark()

        def stage0():
            A.reset()
            pb = PB("s0ps")
            xin = Ring(A, 2, [D], F32, "xin")
            xst = Ring(A, 2, [16, 128], F32, "xst")
            xT_v = xT.rearrange("(c p) t -> p c t", p=128)
            k = 0
            for tti in range(32):
                xt, bx = xin.next()
                P.dma("sp", xt, x_in[tti * 128:(tti + 1) * 128, :], [], [bx], bx)
                so, bs = xst.next()
                for g in range(4):
                    bk = banks[k % 8]
                    bb = pb[k % 8]
                    k += 1
                    for j in range(4):
                        c = g * 4 + j
                        tr(bk[:, j * 128:(j + 1) * 128], xt[:, c * 128:(c + 1) * 128], ident, [bx, b_cst], [bb])
                    cp("act" if g % 2 else "dve", so[:, g * 4:(g + 1) * 4, :],
                       bk[:, :].rearrange("p (a b) -> p a b", b=128), [bb], [bs])
                P.dma("pool", xT_v[:, :, tti * 128:(tti + 1) * 128], so, [bs], [], bs)
            P.barrier()

        def stageA(l):
            w_v = w_in[l].rearrange("(c p) n -> p c n", p=128)
            w1 = nrm_sb[:, l, :]
            chunks = []
            for (wo, n, ro, sc) in FM_GROUPS:
                for j in range(n):
                    chunks.append((wo + j * 128, ro + j * 128, sc))
            slabs = [(512, 512, 0, True), (1024, 512, 512, False), (1536, 512, 1024, False),
                     (3584, 512, 1536, False), (4096, 256, 2048, False), (5888, 256, 2304, False),
                     (6144, 512, 2560, False)]
            for half in range(2):
                A.reset()
                pb = PB("Aps")
                TH = 2048
                tb = half * TH
                hT = A.alloc([16, TH], BF16)
                b_hT = [Buf("hT%d" % i) for i in range(4)]
                rrow = A.alloc([TH], F32)
                b_rrow = [Buf("rrow%d" % i) for i in range(4)]
                rcol = A.alloc([16], F32)
                rcolk = A.alloc([16], F32)
                b_rcol = Buf("rcol")
                xs_r = Ring(A, 4, [512], F32, "xs")
                sq_r = Ring(A, 3, [512], BF16, "sq")
                tmp_r = Ring(A, 2, [512], F32, "tmpA")
                for blk in range(4):
                    sl = slice(blk * 512, (blk + 1) * 512)
                    gsl = slice(tb + blk * 512, tb + (blk + 1) * 512)
                    bk, bb = banks[blk % 2], pb[blk % 2]
                    for c in range(16):
                        xs, bx = xs_r.next()
                        P.dma("sp", xs, xT[c * 128:(c + 1) * 128, gsl], [], [bx], bx)
                        sq, bs = sq_r.next()
                        act(sq, xs, AF.Square, [bx], [bs])
                        mm(bk[:, :], ones128, sq, c == 0, c == 15, [bs, b_ones], [bb])
                        ts("dve", hT[:, c, sl], xs, w1[:, c:c + 1], None, ALU.mult, None, [bx, b_nrm], [b_hT[blk]])
                    tm, bt = tmp_r.next()
                    act(tm, bk[:, :], AF.Sqrt, [bb, b_ones], [bt], bias=eps_col, scale=1.0 / D)
                    recip(rrow[:, sl], tm, [bt], [b_rrow[blk]])
                for tti in range(16):
                    bk, bb = banks[2 + tti % 2], pb[2 + tti % 2]
                    tr(bk[:, 0:128], rrow[:, tti * 128:(tti + 1) * 128], ident, [b_rrow[tti // 4], b_cst], [bb])
                    cp("dve", rcol[:, tti:tti + 1], bk[:, 0:1], [bb], [b_rcol])
                ts("dve", rcolk, rcol, 0.125, None, ALU.mult, None, [b_rcol], [b_rcol])
                wc_r = Ring(A, 3, [16, 128], BF16, "wc")
                stg_r = Ring(A, 4, [2048], F32, "stgA")
                st_r = Ring(A, 2, [TH], BF16, "stA2")
                ws_r = Ring(A, 2, [16, 512], BF16, "ws")
                st3_r = Ring(A, 3, [512], BF16, "stA3")
                work = [("fm",) + ch for ch in chunks] + [("tm",) + sl_ for sl_ in slabs]
                kk = [0]

                def a_load(item):
                    if item[0] == "fm":
                        wc, bw = wc_r.next()
                        load_cast(stg_r, "sp", "act", wc, bw, w_v[:, :, item[1]:item[1] + 128])
                        return (wc, bw)
                    _, wo, wn, to, ks = item
                    wsb, bw = ws_r.next()
                    for pc in range(wn // 128):
                        load_cast(stg_r, "sp", "act", wsb[:, :, pc * 128:(pc + 1) * 128], bw,
                                  w_v[:, :, wo + pc * 128:wo + (pc + 1) * 128])
                    return (wsb, bw)

                def a_compute(item, wt):
                    if item[0] == "fm":
                        _, wo, ro, sc = item
                        wc, bw = wt
                        so, bs = st_r.next()
                        for blk in range(4):
                            sl = slice(blk * 512, (blk + 1) * 512)
                            bk, bb = banks[4 + kk[0] % 4], pb[4 + kk[0] % 4]
                            kk[0] += 1
                            for c in range(16):
                                mm(bk[:, :], wc[:, c, :], hT[:, c, sl], c == 0, c == 15, [bw, b_hT[blk]], [bb])
                            stt("dve", so[:, sl], bk[:, :], sc, rrow[:, sl], ALU.mult, ALU.mult, [bb, b_rrow[blk]], [bs])
                        P.dma("pool", qkT[ro:ro + 128, tb:tb + TH], so, [bs], [], bs)
                        return
                    _, wo, wn, to, ks = item
                    wsb, bw = wt
                    for tti in range(16):
                        bk, bb = banks[kk[0] % 4], pb[kk[0] % 4]
                        kk[0] += 1
                        for c in range(16):
                            mm(bk[:, 0:wn], hT[:, c, tti * 128:(tti + 1) * 128], wsb[:, c, 0:wn], c == 0, c == 15,
                               [bw, b_hT[tti // 4]], [bb])
                        so, bs = st3_r.next()
                        amul(so[:, 0:wn], bk[:, 0:wn], (rcolk if ks else rcol)[:, tti:tti + 1], [bb, b_rcol], [bs])
                        g0 = tb + tti * 128
                        P.dma("pool", tokm[g0:g0 + 128, to:to + wn], so[:, 0:wn], [bs], [], bs)

                DEPTH = 1
                wts = []
                for i in range(len(work) + DEPTH):
                    if i < len(work):
                        wts.append(a_load(work[i]))
                    if i >= DEPTH:
                        a_compute(work[i - DEPTH], wts[i - DEPTH])
                P.barrier()

        ones128 = A.alloc([128], BF16)
        b_ones = Buf("ones")
        mset("dve", ones128, 1.0, [b_ones])
        eps_col = A.alloc([1], F32)
        mset("dve", eps_col, EPS, [b_ones])
        A.mark()

        def stageB1(l):
            A.reset()
            pb = PB("B1ps")
            dec = A.alloc([24], F32)
            b_dec = Buf("dec")
            P.dma("sp", dec, rdec[l], [], [b_dec], b_dec)
            nlg = A.alloc([24], F32)
            tA = A.alloc([24], F32)
            tB = A.alloc([24], F32)
            b_c = Buf("B1c")
            act(tA, dec, AF.Exp, [b_dec], [b_c], scale=-1.0)
            ts("dve", tB, tA, -0.2, 0.25, ALU.mult, ALU.add, [b_c], [b_c])
            tt("dve", tB, tB, tA, ALU.mult, [b_c], [b_c])
            ts("dve", tB, tB, -1.0, 1.0 / 3.0, ALU.mult, ALU.add, [b_c], [b_c])
            tt("dve", tB, tB, tA, ALU.mult, [b_c], [b_c])
            ts("dve", tB, tB, -1.0, 0.5, ALU.mult, ALU.add, [b_c], [b_c])
            tt("dve", tB, tB, tA, ALU.mult, [b_c], [b_c])
            ts("dve", tB, tB, -1.0, 1.0, ALU.mult, ALU.add, [b_c], [b_c])
            tt("dve", nlg, tB, tA, ALU.mult, [b_c], [b_c])
            nlgf_b, nlgb_b = nlg[:, 0:8], nlg[:, 8:16]
            nlgf_p, nlgb_p = nlg[:, 16:20], nlg[:, 20:24]
            DT = A.alloc([8, 128], F32)
            e2 = A.alloc([128], F32)
            for h in range(8):
                act(DT[:, h, :], cst_sb[:, C_NIDXF:C_NIDXF + 128], AF.Exp, [b_c, b_cst], [b_c], scale=nlgf_b[:, h:h + 1])
                act(e2, cst_sb[:, C_NIDXB:C_NIDXB + 128], AF.Exp, [b_c, b_cst], [b_c], scale=nlgb_b[:, h:h + 1])
                tt("dve", DT[:, h, :], DT[:, h, :], e2, ALU.add, [b_c], [b_c])
            QDF = A.alloc([4, 128], F32)
            QDB = A.alloc([4, 128], F32)
            for pr in range(4):
                act(QDF[:, pr, :], cst_sb[:, C_NPOS1:C_NPOS1 + 128], AF.Exp, [b_c, b_cst], [b_c], scale=nlgf_p[:, pr:pr + 1])
                act(QDB[:, pr, :], cst_sb[:, C_NPOSB:C_NPOSB + 128], AF.Exp, [b_c, b_cst], [b_c], scale=nlgb_p[:, pr:pr + 1])
            KDF = A.alloc([8], F32)
            KDB = A.alloc([8], F32)
            act(KDF, nlgf_b, AF.Exp, [b_c, b_cst], [b_c], scale=cst_sb[:, C_NK1:C_NK1 + 1])
            act(KDB, nlgb_b, AF.Exp, [b_c, b_cst], [b_c], scale=cst_sb[:, C_NK0:C_NK0 + 1])
            CD = A.alloc([8], F32)
            act(CD, nlg[:, 16:24], AF.Exp, [b_c], [b_c], scale=-128.0)
            CDFB = A.alloc([4, 64], F32)
            CDBB = A.alloc([4, 64], F32)
            cp("dve", CDFB, CD[:, 0:4].unsqueeze(2).to_broadcast([128, 4, 64]), [b_c], [b_c])
            cp("dve", CDBB, CD[:, 4:8].unsqueeze(2).to_broadcast([128, 4, 64]), [b_c], [b_c])
            nw = A.alloc([512], F32)
            b_nw = Buf("rnw")
            P.dma("sp", nw, rnw[l], [], [b_nw], b_nw)
            LV = int(os.environ.get("B1LV", "9"))
            qt_r = Ring(A, 3, [2, 4, 128], BF16, "QKT")
            qk_v = qkT[0:1024, :].rearrange("(g c p) t -> p g c t", p=128, g=2)
            tk_r = Ring(A, 4, [1536], BF16, "TK")
            tk_v = tokm[:, 0:1536].rearrange("(c p) f -> p c f", p=128)
            Sf_all = A.alloc([32, 256], BF16)
            Sb_all = A.alloc([32, 256], BF16)
            Ub_all = A.alloc([32, 256], F32)
            b_Sf = [Buf("Sf%d" % i) for i in range(32)]
            b_Sb = [Buf("Sb%d" % i) for i in range(32)]
            b_Ub = [Buf("Ub%d" % i) for i in range(32)]
            S = A.alloc([256], F32)
            b_S = Buf("S")
            kd_r = Ring(A, 2, [2, 8, 64], BF16, "kd")
            tmpS = A.alloc([256], F32)
            mset("dve", S, 0.0, [b_S])
            for c in range(32):
                kd, bkd = kd_r.next()
                tkc, btk = tk_r.next()
                P.dma("sp", tkc[:, 0:1024], tk_v[:, c, 0:1024], [], [btk], btk)
                rk_c = tkc[:, 0:512].rearrange("p (h d) -> p h d", d=64)
                rv_c = tkc[:, 512:1024].rearrange("p (h d) -> p h d", d=64)
                tt("pool", kd[:, 0, :, :], rk_c, KDF.unsqueeze(2).to_broadcast([128, 8, 64]), ALU.mult, [btk, b_c], [bkd])
                tt("pool", kd[:, 1, :, :], rk_c, KDB.unsqueeze(2).to_broadcast([128, 8, 64]), ALU.mult, [btk, b_c], [bkd])
                bkf, bbf = banks[(2 * c) % 4], pb[(2 * c) % 4]
                bkb, bbb = banks[(2 * c + 1) % 4], pb[(2 * c + 1) % 4]
                for h in range(8):
                    po = (h % 2) * 64
                    mm(bkf[po:po + 64, (h // 2) * 64:(h // 2) * 64 + 64], kd[:, 0, h, :], rv_c[:, h, :], True, True,
                       [bkd, btk], [bbf])
                    mm(bkb[po:po + 64, (h // 2) * 64:(h // 2) * 64 + 64], kd[:, 1, h, :], rv_c[:, h, :], True, True,
                       [bkd, btk], [bbb])
                cp("act", Ub_all[:, c, :], bkb[:, 0:256], [bbb], [b_Ub[c]])
                if c == 16:
                    ts("dve", S, S, keep_col, None, ALU.mult, None, [b_S, b_flg], [b_S])
                cp("dve", Sf_all[:, c, :], S, [b_S], [b_Sf[c]])
                tt("dve", tmpS, S, CDFB.rearrange("p a b -> p (a b)"), ALU.mult, [b_S, b_c], [b_S])
                tt("dve", S, tmpS, bkf[:, 0:256], ALU.add, [b_S, bbf], [b_S])
            Sb = A.alloc([256], F32)
            b_Sbr = Buf("Sbr")
            mset("dve", Sb, 0.0, [b_Sbr])
            for c in range(31, -1, -1):
                if c == 15:
                    ts("dve", Sb, Sb, keep_col, None, ALU.mult, None, [b_Sbr, b_flg], [b_Sbr])
                cp("dve", Sb_all[:, c, :], Sb, [b_Sbr], [b_Sb[c]])
                tt("dve", tmpS, Sb, CDBB.rearrange("p a b -> p (a b)"), ALU.mult, [b_Sbr, b_c], [b_Sbr])
                tt("dve", Sb, tmpS, Ub_all[:, c, :], ALU.add, [b_Sbr, b_Ub[c]], [b_Sbr])
            MT_r = Ring(A, 2, [8, 128], BF16, "MT")
            qd_r = Ring(A, 2, [2, 4, 128], BF16, "qd")
            o_r = Ring(A, 2, [8, 64], F32, "oret")
            sg_r = Ring(A, 2, [512], F32, "sg")
            st4 = Ring(A, 2, [8, 4], F32, "stat")
            yb_r = Ring(A, 2, [512], BF16, "yb")
            ot_r = Ring(A, 3, [4, 128], BF16, "outT")
            mix_ret_v = mixT[0:512, :].rearrange("(c p) t -> p c t", p=128)
            st2 = {}

            def P1(c):
                gcs = slice(c * 128, (c + 1) * 128)
                qkc, b_QT = qt_r.next()
                P.dma("sp", qkc, qk_v[:, :, :, gcs], [], [b_QT], b_QT)
                QT = qkc[:, 0, :, :]
                KT = qkc[:, 1, :, :]
                MT, bMT = MT_r.next()
                for hh in range(4):
                    for par in range(2):
                        po = par * 64
                        mm(banks[4 + par][:, hh * 128:(hh + 1) * 128], KT[po:po + 64, hh, :], QT[po:po + 64, hh, :], True, True,
                           [b_QT], [pb[4 + par]])
                for par in range(2):
                    tt("dve", MT[:, par:8:2, :], banks[4 + par][:, :].rearrange("p (a b) -> p a b", b=128),
                       DT[:, par:8:2, :], ALU.mult, [pb[4 + par], b_c], [bMT])
                qd, bqd = qd_r.next()
                tt("pool", qd[:, 0, :, :], QT, QDF, ALU.mult, [b_QT, b_c], [bqd])
                tt("pool", qd[:, 1, :, :], QT, QDB, ALU.mult, [b_QT, b_c], [bqd])
                st2[c] = (MT, bMT, qd, bqd)

            def P2(c):
                MT, bMT, qd, bqd = st2.pop(c)
                bk, bb = banks[6 + c % 2], pb[6 + c % 2]
                tkc, btk = tk_r.next()
                P.dma("sp", tkc[:, 512:1536], tk_v[:, c, 512:1536], [], [btk], btk)
                rv_c = tkc[:, 512:1024].rearrange("p (h d) -> p h d", d=64)
                for h in range(8):
                    po = (h % 2) * 64
                    pr = h // 2
                    o_ps = bk[:, h * 64:(h + 1) * 64]
                    mm(o_ps, MT[:, h, :], rv_c[:, h, :], True, False, [bMT, btk], [bb])
                    mm(o_ps, qd[po:po + 64, 0, pr, :], Sf_all[po:po + 64, c, pr * 64:(pr + 1) * 64], False, False,
                       [bqd, b_Sf[c]], [bb])
                    mm(o_ps, qd[po:po + 64, 1, pr, :], Sb_all[po:po + 64, c, pr * 64:(pr + 1) * 64], False, True,
                       [bqd, b_Sb[c]], [bb])
                o, bo = o_r.next()
                sv, bsv = st4.next()
                o_v = bk[:, :].rearrange("p (h d) -> p h d", d=64)
                rsum("dve", sv[:, :, 0], o_v, [bb], [bsv])
                ts("dve", sv[:, :, 1], sv[:, :, 0], -1.0 / 64.0, None, ALU.mult, None, [bsv], [bsv])
                tt("dve", o, o_v, sv[:, :, 1:2].to_broadcast([128, 8, 64]), ALU.add, [bb, bsv], [bo])
                sg, bsg = sg_r.next()
                tt("pool", sg.rearrange("p (h d) -> p h d", d=64), o, o, ALU.mult, [bo], [bsg])
                rsum("dve", sv[:, :, 2], sg.rearrange("p (h d) -> p h d", d=64), [bsg], [bsv])
                act(sv[:, :, 3], sv[:, :, 2], AF.Sqrt, [bsv, b_ones], [bsv], bias=eps_col, scale=1.0 / 64.0)
                recip(sv[:, :, 2], sv[:, :, 3], [bsv], [bsv])
                tt("dve", o, o, sv[:, :, 2:3].to_broadcast([128, 8, 64]), ALU.mult, [bo, bsv], [bo])
                act(sg, tkc[:, 1024:1536], AF.Silu, [btk, bsg], [bsg])
                tt("pool", sg, sg, nw, ALU.mult, [bsg, b_nw], [bsg])
                yb, byb = yb_r.next()
                tt("dve", yb, o.rearrange("p h d -> p (h d)"), sg, ALU.mult, [bo, bsg], [byb])
                st2[("y", c)] = (yb, byb)

            def P3(c):
                yb, byb = st2.pop(("y", c))
                bk2, bb2 = banks[c % 2], pb[c % 2]
                pbf = bk2[:, :].bitcast(BF16)
                for j in range(4):
                    tr(pbf[:, j * 128:(j + 1) * 128], yb[:, j * 128:(j + 1) * 128], ident_bf, [byb, b_idb], [bb2])
                ot, bot = ot_r.next()
                cp("act", ot, pbf[:, 0:512].rearrange("p (a b) -> p a b", b=128), [bb2], [bot])
                P.dma("sp", mix_ret_v[:, :, c * 128:(c + 1) * 128], ot, [bot], [], bot)

            P1(0)
            for c in range(32):
                if c + 1 < 32:
                    P1(c + 1)
                P2(c)
                if c >= 1:
                    P3(c - 1)
            P3(31)
            P.barrier()

        def stageB2(l):
            A.reset()
            pb = PB("B2ps")
            q_r = Ring(A, 2, [T], BF16, "naq")
            k_r = Ring(A, 2, [T], BF16, "nak")
            vr_r = Ring(A, 2, [33, 128], BF16, "navraw")
            va_r = Ring(A, 2, [32, 2, 65], BF16, "nava")
            vb_r = Ring(A, 2, [32, 2, 65], BF16, "navb")
            tt_r = Ring(A, 2, [2, 15, 64], F32, "natt")
            sb_r = Ring(A, 2, [2, 6, 64], F32, "nasb")
            pt_r = Ring(A, 3, [2, 6, 64], BF16, "napt")
            rc_r = Ring(A, 2, [2], F32, "narc")
            ob_r = Ring(A, 2, [128], BF16, "naob")
            ot_r = Ring(A, 2, [T], BF16, "naot")
            for (v, b) in va_r.items + vb_r.items:
                mset("pool", v, 1.0, [b])
            def load_pair(pair):
                qT, bq = q_r.next()
                kT, bk_ = k_r.next()
                P.dma("sp", qT, qkT[1024 + pair * 128:1024 + (pair + 1) * 128, :], [], [bq], bq)
                P.dma("sp", kT, qkT[1792 + pair * 128:1792 + (pair + 1) * 128, :], [], [bk_], bk_)
                vr, bvr = vr_r.next()
                c0 = 1536 + pair * 128
                P.dma("sp", vr[:, 0:32, :], tokm[:, c0:c0 + 128].rearrange("(c p) f -> p c f", p=128), [], [bvr], bvr)
                va, bva = va_r.next()
                cp("pool", va[:, :, :, 0:64], vr[:, 0:32, :].rearrange("p c (h d) -> p c h d", d=64), [bvr], [bva])
                vr2, bvr2 = vr_r.next()
                P.dma("sp", vr2[:, 0:31, :], tokm[64:64 + 31 * 128, c0:c0 + 128].rearrange("(c p) f -> p c f", p=128),
                      [], [bvr2], bvr2)
                vb, bvb = vb_r.next()
                cp("pool", vb[:, 0:31, :, 0:64], vr2[:, 0:31, :].rearrange("p c (h d) -> p c h d", d=64), [bvr2], [bvb])
                ttb, btt = tt_r.next()
                P.dma("sp", ttb.rearrange("p a b c -> p (a b c)"), tt_in[l, pair], [], [btt], btt)
                oT, boT = ot_r.next()
                return dict(qT=qT, bq=bq, kT=kT, bk_=bk_, va=va, bva=bva, vb=vb, bvb=bvb, ttb=ttb, btt=btt, oT=oT, boT=boT)

            tiles = {0: load_pair(0)}
            steps = [(p_, rq_) for p_ in range(6) for rq_ in range(64)]
            pts = {}

            def score_banks(rq):
                kk = rq % 2
                return [(banks[2 * kk], pb[2 * kk]), (banks[2 * kk + 1], pb[2 * kk + 1])]

            def QK(j):
                p_, rq = steps[j]
                t_ = tiles[p_]
                ws, nk, slot = NA_PLAN[rq]
                for hh in range(2):
                    bk, bb = score_banks(rq)[hh]
                    for m in range(nk):
                        k0 = ws * 64 + 128 * m
                        mm(bk[:, m * 64:(m + 1) * 64], t_["kT"][hh * 64:hh * 64 + 64, k0:k0 + 128],
                           t_["qT"][hh * 64:hh * 64 + 64, rq * 64:(rq + 1) * 64], True, True, [t_["bk_"], t_["bq"]], [bb])

            def EW(j):
                p_, rq = steps[j]
                t_ = tiles[p_]
                ws, nk, slot = NA_PLAN[rq]
                sbt, bsb = sb_r.next()
                a0 = ws - rq + 7
                for hh in range(2):
                    bk, bb = score_banks(rq)[hh]
                    tt("dve", sbt[:, hh, 0:nk, :], bk[:, 0:nk * 64].rearrange("p (m q) -> p m q", q=64),
                       t_["ttb"][:, hh, a0:a0 + 2 * nk - 1:2, :], ALU.add, [bb, t_["btt"]], [bsb])
                pt, bpt = pt_r.next()
                if slot is None:
                    act(pt[:, :, 0:nk, :], sbt[:, :, 0:nk, :], AF.Exp, [bsb], [bpt])
                else:
                    for m in range(nk):
                        act(pt[:, :, m, :], sbt[:, :, m, :], AF.Exp, [bsb, b_flg], [bpt],
                            bias=flg_sb[:, 2 + slot + m:3 + slot + m])
                pts[j] = (pt, bpt)

            def PV(j):
                p_, rq = steps[j]
                t_ = tiles[p_]
                ws, nk, slot = NA_PLAN[rq]
                tti, par = divmod(rq, 2)
                bko, bbo = banks[4 + tti % 2], pb[4 + tti % 2]
                pt, bpt = pts.pop(j)
                for hh in range(2):
                    for m in range(nk):
                        if ws % 2 == 0:
                            vt, bv = t_["va"][:, ws // 2 + m, hh, :], t_["bva"]
                        else:
                            vt, bv = t_["vb"][:, (ws - 1) // 2 + m, hh, :], t_["bvb"]
                        mm(bko[par * 64:par * 64 + 64, hh * 65:(hh + 1) * 65], pt[:, hh, m, :], vt,
                           m == 0, m == nk - 1, [bpt, bv], [bbo])
                if par == 1:
                    rc, brc = rc_r.next()
                    o_v = bko[:, 0:130].rearrange("p (h d) -> p h d", d=65)
                    recip(rc, o_v[:, :, 64], [bbo], [brc])
                    ob, bob = ob_r.next()
                    tt("dve", ob.rearrange("p (h d) -> p h d", d=64), o_v[:, :, 0:64],
                       rc.unsqueeze(2).to_broadcast([128, 2, 64]), ALU.mult, [bbo, brc], [bob])
                    bk2, bb2 = banks[6 + tti % 2], pb[6 + tti % 2]
                    pbf = bk2[:, :].bitcast(BF16)
                    tr(pbf[:, 0:128], ob, ident_bf, [bob, b_idb], [bb2])
                    cp("act", t_["oT"][:, tti * 128:(tti + 1) * 128], pbf[:, 0:128], [bb2], [t_["boT"]])
                if rq == 63:
                    P.dma("sp", mixT[512 + p_ * 128:512 + (p_ + 1) * 128, :], t_["oT"], [t_["boT"]], [], t_["boT"])

            QK(0)
            QK(1)
            for j, (p_, rq) in enumerate(steps):
                if rq == 0 and p_ + 1 < 6:
                    tiles[p_ + 1] = load_pair(p_ + 1)
                EW(j)
                if j + 2 < len(steps):
                    QK(j + 2)
                PV(j)
            P.barrier()

        def stageB3(l):
            A.reset()
            pb = PB("B3ps")
            lam_init = 0.8 - 0.6 * math.exp(-0.3 * l)
            lq = A.alloc([256], F32)
            b_l = Buf("lam")
            P.dma("sp", lq, dlam[l], [], [b_l], b_l)
            lt = A.alloc([128], F32)
            lv = A.alloc([4], F32)
            tt("dve", lt[:, 0:64], lq[:, 0:64], lq[:, 64:128], ALU.mult, [b_l], [b_l])
            tt("dve", lt[:, 64:128], lq[:, 128:192], lq[:, 192:256], ALU.mult, [b_l], [b_l])
            rsum("dve", lv[:, 0:2], lt.rearrange("p (a b) -> p a b", b=64), [b_l], [b_l])
            act(lv[:, 2:4], lv[:, 0:2], AF.Exp, [b_l], [b_l])
            tt("dve", lv[:, 0:1], lv[:, 3:4], lv[:, 2:3], ALU.subtract, [b_l], [b_l])
            ts("dve", lv[:, 1:2], lv[:, 0:1], -lam_init, None, ALU.add, None, [b_l], [b_l])
            neglam = lv[:, 1:2]
            nw = A.alloc([768], F32)
            b_nw = Buf("dnw")
            P.dma("sp", nw, dnw[l], [], [b_nw], b_nw)
            ts("dve", nw, nw, 1.0 - lam_init, None, ALU.mult, None, [b_nw], [b_nw])
            cB = A.alloc([CW - C_IOTA], F32)
            b_cB = Buf("cB")
            P.dma("sp", cB, cst[:, C_IOTA:CW], [], [b_cB], b_cB)
            O_IOTA, O_NEAR, O_BCA, O_BCB = 0, C_NEAR - C_IOTA, C_BCA - C_IOTA, C_BCB - C_IOTA
            BCX = A.alloc([192], F32)
            BCBX = A.alloc([192], F32)
            b_bcx = Buf("bcx")
            ts("dve", BCX, cB[:, O_BCA:O_BCA + 192], xneg_col, None, ALU.add, None, [b_cB, b_flg], [b_bcx])
            ts("dve", BCBX, cB[:, O_BCB:O_BCB + 192], xneg_col, None, ALU.add, None, [b_cB, b_flg], [b_bcx])
            zero_col = A.alloc([1], F32)
            mset("dve", zero_col, 0.0, [b_bcx])
            q_r = Ring(A, 2, [2, T], BF16, "dq")
            k_r = Ring(A, 2, [2, T], BF16, "dk")
            for (kt_, kb_) in k_r.items:
                for c in range(2):
                    cp("pool" if c else "dve", kt_[64:128, c, :], cst_sb[64:128, C_KAUG:C_KAUG + 1].to_broadcast([64, T]), [b_cst], [kb_])
            vr_r = Ring(A, 2, [32, 128], BF16, "dvraw")
            v1_r = Ring(A, 2, [32, 129], BF16, "dv1")
            sb_r = Ring(A, 2, [2, 512], F32, "dsb")
            pt_r = Ring(A, 3, [2, 512], BF16, "dpt")
            sm_r = Ring(A, 2, [8], F32, "dsm")
            o_r = Ring(A, 2, [128], F32, "do")
            sq_r = Ring(A, 2, [128], F32, "dsq")
            ob_r = Ring(A, 2, [128], BF16, "dob")
            ot_r = Ring(A, 2, [T], BF16, "dot")
            for (v, b) in v1_r.items:
                mset("pool", v, 1.0, [b])
            iota2 = cB[:, O_IOTA:O_IOTA + 512].unsqueeze(1).to_broadcast([128, 2, 512])
            asb_r = Ring(A, 2, [3, 387], F32, "dasb")
            fz_r = Ring(A, 2, [4, 8], F32, "dfz")
            oa_r = Ring(A, 2, [4, 128], F32, "doa")
            sq4_r = Ring(A, 1, [4, 128], F32, "dsq4")
            ob8_r = Ring(A, 8, [128], BF16, "dob8")
            accs = []
            for c in range(2):
                for qs in range(4):
                    idx = c * 4 + qs
                    accs.append((banks[4 + idx // 3][:, (idx % 3) * 129:(idx % 3) * 129 + 129], pb[4 + idx // 3], idx))
            for h in range(6):
                qT, bq = q_r.next()
                kT, bk_ = k_r.next()
                for c in range(2):
                    r0 = h * 128 + c * 64
                    P.dma("sp", qT[0:64, c, :], qkT[2560 + r0:2560 + r0 + 64, :], [], [bq], bq)
                    P.dma("sp", kT[0:64, c, :], qkT[3328 + r0:3328 + r0 + 64, :], [], [bk_], bk_)
                    asrc = augq[h].unsqueeze(1).to_broadcast([4, 8, 512])
                    P.dma("pool", qT[64:68, c, :].rearrange("p (a b) -> p a b", b=512), asrc, [], [bq], bq)
                vr, bvr = vr_r.next()
                c0 = 2304 + h * 128
                P.dma("sp", vr, tokm[:, c0:c0 + 128].rearrange("(c p) f -> p c f", p=128), [], [bvr], bvr)
                v1, bv1 = v1_r.next()
                cp("pool", v1[:, :, 0:128], vr, [bvr], [bv1])
                oT, boT = ot_r.next()
                slope = SLOPES[h]
                pts = {}
                pend = []
                blocks = []
                for qb in range(8):
                    lst = []
                    for kt in range(32):
                        d = kt - 4 * qb
                        if d >= 4:
                            dmin = 128 * d - 511
                        elif d < 0:
                            dmin = 128 * (-d) - 127
                        else:
                            dmin = 0
                        if slope * dmin >= 200.0:
                            continue
                        lst.append(kt)
                    for kt in lst:
                        blocks.append((qb, kt, kt == lst[0], kt == lst[-1]))
                NB = len(blocks)

                def QK(j):
                    qb, kt, first, last = blocks[j]
                    b0 = 2 * (j % 2)
                    d = kt - 4 * qb
                    win = slice(0, 66) if d >= 4 else (slice(0, 68) if d < 0 else slice(0, 64))
                    for c in range(2):
                        mm(banks[b0 + c], kT[win, c, kt * 128:(kt + 1) * 128],
                           qT[win, c, qb * 512:(qb + 1) * 512], True, True, [bk_, bq], [pb[b0 + c]])

                def EW(j):
                    qb, kt, first, last = blocks[j]
                    b0 = 2 * (j % 2)
                    cross = (kt // 16) != (qb // 4)
                    d = kt - 4 * qb
                    rd = [b_cB, b_bcx]
                    pt, bpt = pt_r.next()
                    s2 = psum_all[:, b0 * 512:(b0 + 2) * 512].rearrange("p (c n) -> p c n", n=512)
                    if d >= 4:
                        col = (BCX if cross else cB[:, O_BCA:O_BCA + 192])[:, h * 32 + d:h * 32 + d + 1]
                        act(pt, s2, AF.Exp, [pb[b0], pb[b0 + 1]] + rd, [bpt], bias=col)
                    elif d < 0:
                        dd = -d
                        col = (BCBX if cross else cB[:, O_BCB:O_BCB + 192])[:, h * 32 + dd:h * 32 + dd + 1]
                        act(pt, s2, AF.Exp, [pb[b0], pb[b0 + 1]] + rd, [bpt], bias=col)
                    else:
                        sbt, bsb = sb_r.next()
                        in0 = cB[:, O_NEAR + d * 512:O_NEAR + (d + 1) * 512].unsqueeze(1).to_broadcast([128, 2, 512])
                        stt("dve", sbt, in0, slope, s2, ALU.mult, ALU.add, [pb[b0], pb[b0 + 1]] + rd, [bsb])
                        act(pt, sbt, AF.Exp, [bsb] + rd, [bpt])
                    pts[j] = (pt, bpt)

                def PV(j):
                    qb, kt, first, last = blocks[j]
                    pt, bpt = pts.pop(j)
                    for c in range(2):
                        for qs in range(4):
                            ap_, bb_, idx = accs[c * 4 + qs]
                            mm(ap_, pt[:, c, qs * 128:(qs + 1) * 128], v1[:, kt, :],
                               first and idx % 3 == 0, last, [bpt, bv1], [bb_])
                    if last:
                        finalize(qb, j)

                def finalize(qb, j):
                    asb, basb = asb_r.next()
                    for b in range(3):
                        cp("dve", asb[:, b, :], banks[4 + b][:, 0:387], [pb[4 + b]], [basb])
                    fz, bfz = fz_r.next()
                    oa, boa = oa_r.next()
                    for qs in range(4):
                        i0_, i1_ = qs, 4 + qs
                        a0_ = asb[:, i0_ // 3, (i0_ % 3) * 129:(i0_ % 3) * 129 + 129]
                        a1_ = asb[:, i1_ // 3, (i1_ % 3) * 129:(i1_ % 3) * 129 + 129]
                        recip(fz[:, qs, 0:1], a0_[:, 128:129], [basb], [bfz])
                        recip(fz[:, qs, 1:2], a1_[:, 128:129], [basb], [bfz])
                        tt("dve", fz[:, qs, 2:3], fz[:, qs, 1:2], neglam, ALU.mult, [bfz, b_l], [bfz])
                        ts("dve", oa[:, qs, :], a0_[:, 0:128], fz[:, qs, 0:1], None, ALU.mult, None, [basb, bfz], [boa])
                        stt("dve", oa[:, qs, :], a1_[:, 0:128], fz[:, qs, 2:3], oa[:, qs, :], ALU.mult, ALU.add,
                            [basb, bfz, boa], [boa])
                    sq4, bsq4 = sq4_r.next()
                    tt("dve", sq4, oa, oa, ALU.mult, [boa], [bsq4])
                    rsum("dve", fz[:, :, 3], sq4, [bsq4], [bfz])

                    def F2(fz=fz, bfz=bfz):
                        act(fz[:, :, 4], fz[:, :, 3], AF.Ln, [bfz, b_ones], [bfz], bias=eps_col, scale=1.0 / 128.0)
                        act(fz[:, :, 5], fz[:, :, 4], AF.Exp, [bfz], [bfz], scale=-0.5)

                    def F3(fz=fz, bfz=bfz, oa=oa, boa=boa, qb=qb):
                        for qs in range(4):
                            ob, bob = ob8_r.next()
                            stt("dve", ob, oa[:, qs, :], fz[:, qs, 5:6], nw[:, h * 128:(h + 1) * 128], ALU.mult, ALU.mult,
                                [boa, bfz, b_nw], [bob])
                            tti = qb * 4 + qs

                            def fin(ob=ob, bob=bob, tti=tti):
                                bk2, bb2 = banks[7], pb[7]
                                pbf = bk2[:, :].bitcast(BF16)
                                tr(pbf[:, (tti % 2) * 128:(tti % 2) * 128 + 128], ob, ident_bf, [bob, b_idb], [bb2])
                                cp("dve", oT[:, tti * 128:(tti + 1) * 128], pbf[:, (tti % 2) * 128:(tti % 2) * 128 + 128],
                                   [bb2], [boT])
                            pend.append((j + 10, fin))
                    pend.append((j + 3, F2))
                    pend.append((j + 6, F3))
                    pend.sort(key=lambda t_: t_[0])

                QK(0)
                QK(1)
                for j in range(NB):
                    EW(j)
                    if j + 2 < NB:
                        QK(j + 2)
                    PV(j)
                    pend.sort(key=lambda t_: t_[0])
                    while pend and pend[0][0] <= j:
                        pend.pop(0)[1]()
                        pend.sort(key=lambda t_: t_[0])
                while pend:
                    pend.pop(0)[1]()
                P.dma("sp", mixT[1280 + h * 128:1280 + (h + 1) * 128, :], oT, [boT], [], boT)
            P.barrier()

        def stageC(l):
            A.reset()
            pb = PB("Cps")
            wo = A.alloc([16, D], BF16)
            b_wo = [Buf("wo%d" % i) for i in range(4)]
            w_v = w_out[l].rearrange("(c p) n -> p c n", p=128)
            stg_r = Ring(A, 4, [2048], F32, "stgC")
            for q in range(16):
                load_cast(stg_r, "sp", "pool", wo[:, :, q * 128:(q + 1) * 128], b_wo[q // 4], w_v[:, :, q * 128:(q + 1) * 128])
            mx_r = Ring(A, 2, [16, 512], BF16, "mx")
            xs_r = Ring(A, 3, [512], F32, "xc")
            sq_r = Ring(A, 3, [512], BF16, "sqc")
            hb_r = Ring(A, 3, [512], BF16, "hbc")
            tmp_r = Ring(A, 2, [512], F32, "tmpC")
            rr_r = Ring(A, 2, [512], F32, "rrC")
            zc = A.alloc([16, 32], BF16)
            b_zc = Buf("zc")
            mset("dve", zc, 0.0, [b_zc])
            h2_v = h2T.rearrange("(c p) t -> p c t", p=128)
            P.dma("sp", h2_v[:, :, 0:32], zc, [b_zc], [], b_zc)
            P.dma("sp", h2_v[:, :, T + 32:T + 64], zc, [b_zc], [], b_zc)
            zc32 = zc.rearrange("p a b -> p (a b)")[:, 0:32].bitcast(F32)
            P.dma("sp", rs2[:, 0:16], zc32, [b_zc], [], b_zc)
            P.dma("sp", rs2[:, T + 16:T + 32], zc32, [b_zc], [], b_zc)
            P.barrier()
            w2 = nrm_sb[:, 2 + l, :]
            mix_v = mixT.rearrange("(c p) t -> p c t", p=128)
            k = 0
            pend_sq = None
            for blk in range(8):
                sl = slice(blk * 512, (blk + 1) * 512)
                mx, bmx = mx_r.next()
                P.dma("sp", mx, mix_v[:, :, sl], [], [bmx], bmx)
                bks, bbs = banks[6 + blk % 2], pb[6 + blk % 2]
                for oc in range(16):
                    xs, bx = xs_r.next()
                    P.dma("sp", xs, xT[oc * 128:(oc + 1) * 128, sl], [], [bx], bx)
                    bk, bb = banks[k % 4], pb[k % 4]
                    k += 1
                    for c in range(16):
                        mm(bk[:, :], wo[:, c, oc * 128:(oc + 1) * 128], mx[:, c, :], c == 0, c == 15,
                           [b_wo[oc // 4], bmx], [bb])
                    tt("dve", xs, xs, bk[:, :], ALU.add, [bx, bb], [bx])
                    P.dma("pool", xT[oc * 128:(oc + 1) * 128, sl], xs, [bx], [], bx)
                    sq, bs = sq_r.next()
                    act(sq, xs, AF.Square, [bx], [bs])
                    if pend_sq is not None:
                        pend_sq()

                    def _ssq(sq=sq, bs=bs, oc=oc, bks=bks, bbs=bbs):
                        mm(bks[:, :], ones128, sq, oc == 0, oc == 15, [bs, b_ones], [bbs])
                    pend_sq = _ssq
                    hb, bh = hb_r.next()
                    amul(hb, xs, w2[:, oc:oc + 1], [bx, b_nrm], [bh])
                    P.dma("pool", h2T[oc * 128:(oc + 1) * 128, 32 + blk * 512:32 + (blk + 1) * 512], hb, [bh], [], bh)
                pend_sq()
                pend_sq = None
                tm, bt = tmp_r.next()
                act(tm, bks[:, :], AF.Sqrt, [bbs, b_ones], [bt], bias=eps_col, scale=1.0 / D)
                rr, brr = rr_r.next()
                recip(rr, tm, [bt], [brr])
                P.dma("sp", rs2[:, 16 + blk * 512:16 + (blk + 1) * 512], rr, [brr], [], brr)
            P.barrier()

        def stageD(l):
            A.reset()
            pb = PB("Dps")
            TS = 1024
            cw = A.alloc([44, 4], F32)
            b_cw = Buf("cw")
            P.dma("sp", cw, cwb[l], [], [b_cw], b_cw)
            h2 = A.alloc([16, TS + 2], BF16)
            b_h2 = Buf("h2")
            actT = A.alloc([44, TS], BF16)
            b_act = [Buf("act%d" % i) for i in range(2)]
            rst = A.alloc([TS + 2], F32)
            b_rst = Buf("rst")
            wu_r = Ring(A, 2, [2, 16, 128], BF16, "wu")
            wd_r = Ring(A, 3, [22, 128], BF16, "wd")
            stg_r = Ring(A, 2, [2048], F32, "stgD")
            G_r = Ring(A, 1, [514], F32, "G")
            c_r = Ring(A, 1, [512], F32, "cc")
            ge_r = Ring(A, 1, [512], F32, "ge")
            xs_r = Ring(A, 1, [512], F32, "xd")
            h2_v = h2T.rearrange("(c p) t -> p c t", p=128)
            wu_v = w_up[l].rearrange("(c p) n -> p c n", p=128)
            wd_v = w_dn[l].rearrange("(c p) n -> p c n", p=128)
            k = 0
            def load_h2(t0):
                P.dma("sp", h2, h2_v[:, :, 31 + t0:31 + t0 + TS + 2], [], [b_h2], b_h2)
                P.dma("sp", rst, rs2[:, 15 + t0:15 + t0 + TS + 2], [], [b_rst], b_rst)
                if t0 + TS == 2048:
                    ts("dve", h2[:, :, TS + 1:TS + 2], h2[:, :, TS + 1:TS + 2], keep_col, None, ALU.mult, None,
                       [b_h2, b_flg], [b_h2])
                if t0 == 2048:
                    ts("dve", h2[:, :, 0:1], h2[:, :, 0:1], keep_col, None, ALU.mult, None, [b_h2, b_flg], [b_h2])

            load_h2(0)
            for sbk in range(T // TS):
                t0 = sbk * TS
                def up_load(fc):
                    wu, bwu = wu_r.next()
                    load_cast(stg_r, "sp", "act", wu[:, 0, :, :], bwu, wu_v[:, :, fc * 128:(fc + 1) * 128])
                    load_cast(stg_r, "sp", "act", wu[:, 1, :, :], bwu, wu_v[:, :, 5632 + fc * 128:5632 + (fc + 1) * 128])
                    return (wu, bwu)

                def dn_load(step):
                    oc, hf = divmod(step, 2)
                    wdh, bwdh = wd_r.next()
                    for pc in range(2):
                        load_cast(stg_r, "sp", "act", wdh[:, pc * 11:(pc + 1) * 11, :], bwdh,
                                  wd_v[:, hf * 22 + pc * 11:hf * 22 + (pc + 1) * 11, oc * 128:(oc + 1) * 128])
                    return (wdh, bwdh)

                nxt = up_load(0)
                for fc in range(44):
                    wu, bwu = nxt
                    if fc + 1 < 44:
                        nxt = up_load(fc + 1)
                    else:
                        dn_q = [dn_load(0), dn_load(1)]
                    for sub in range(TS // 512):
                        c0 = 1 + sub * 512
                        bka, bba = banks[(2 * k) % 4], pb[(2 * k) % 4]
                        bkg, bbg = banks[(2 * k + 1) % 4], pb[(2 * k + 1) % 4]
                        bkh, bbh = banks[4 + k % 2], pb[4 + k % 2]
                        k += 1
                        for c in range(16):
                            mm(bka[:, :], wu[:, 0, c, :], h2[:, c, c0:c0 + 512], c == 0, c == 15, [bwu, b_h2], [bba])
                        for c in range(16):
                            mm(bkg[:, :], wu[:, 1, c, :], h2[:, c, c0:c0 + 512], c == 0, c == 15, [bwu, b_h2], [bbg])
                        for c in range(16):
                            mm(bkh[:, 0:2], wu[:, 1, c, :], h2[:, c, c0 - 1:c0 + 513:513], c == 0, c == 15, [bwu, b_h2], [bbh])
                        G, bG = G_r.next()
                        tt("dve", G[:, 1:513], bkg[:, :], rst[:, c0:c0 + 512], ALU.mult, [bbg, b_rst], [bG])
                        tt("dve", G[:, 0:514:513], bkh[:, 0:2], rst[:, c0 - 1:c0 + 513:513], ALU.mult, [bbh, b_rst, bG], [bG])
                        cc, bc = c_r.next()
                        ts("dve", cc, G[:, 1:513], cw[:, fc, 1:2], cw[:, fc, 3:4], ALU.mult, ALU.add, [bG, b_cw], [bc])
                        stt("dve", cc, G[:, 0:512], cw[:, fc, 0:1], cc, ALU.mult, ALU.add, [bG, b_cw, bc], [bc])
                        stt("dve", cc, G[:, 2:514], cw[:, fc, 2:3], cc, ALU.mult, ALU.add, [bG, b_cw, bc], [bc])
                        ge, bge = ge_r.next()
                        act(ge, cc, AF.Gelu_apprx_tanh, [bc], [bge])
                        tt("pool", ge, ge, rst[:, c0:c0 + 512], ALU.mult, [bge, b_rst], [bge])
                        tt("dve", actT[:, fc, sub * 512:(sub + 1) * 512], bka[:, :], ge, ALU.mult, [bba, bge], [b_act[sub]])
                if t0 + TS < T:
                    load_h2(t0 + TS)
                for oc in range(16):
                    wdA, bwdA = dn_q.pop(0)
                    wdB, bwdB = dn_q.pop(0)
                    if oc + 1 < 16:
                        dn_q.append(dn_load(2 * oc + 2))
                    for sub in range(TS // 512):
                        sl = slice(t0 + sub * 512, t0 + (sub + 1) * 512)
                        xs, bx = xs_r.next()
                        P.dma("sp", xs, xT[oc * 128:(oc + 1) * 128, sl], [], [bx], bx)
                        bk, bb = banks[6 + k % 2], pb[6 + k % 2]
                        k += 1
                        for fc in range(44):
                            if fc == 22 and sub == TS // 512 - 1 and oc + 1 < 16:
                                dn_q.append(dn_load(2 * oc + 3))
                            wdx, bwdx = (wdA, bwdA) if fc < 22 else (wdB, bwdB)
                            mm(bk[:, :], wdx[:, fc % 22, :], actT[:, fc, sub * 512:(sub + 1) * 512], fc == 0, fc == 43,
                               [bwdx, b_act[sub]], [bb])
                        tt("dve", xs, xs, bk[:, :], ALU.add, [bx, bb], [bx])
                        P.dma("pool", xT[oc * 128:(oc + 1) * 128, sl], xs, [bx], [], bx)
            P.barrier()

        def stageF():
            A.reset()
            pb = PB("Fps")
            xs_all_r = Ring(A, 2, [16, 512], F32, "xf")
            sq_r = Ring(A, 3, [512], BF16, "sqf")
            tmp_r = Ring(A, 2, [512], F32, "tmpF")
            rr_r = Ring(A, 2, [512], F32, "rrF")
            yo_r = Ring(A, 2, [D], F32, "yo")
            wf = nrm_sb[:, 4, :]
            xT_v = xT.rearrange("(c p) t -> p c t", p=128)
            k = 0
            for blk in range(8):
                sl = slice(blk * 512, (blk + 1) * 512)
                xa, bxa = xs_all_r.next()
                P.dma("sp", xa, xT_v[:, :, sl], [], [bxa], bxa)
                bks, bbs = banks[6 + blk % 2], pb[6 + blk % 2]
                for c in range(16):
                    sq, bs = sq_r.next()
                    act(sq, xa[:, c, :], AF.Square, [bxa], [bs])
                    mm(bks[:, :], ones128, sq, c == 0, c == 15, [bs, b_ones], [bbs])
                tm, bt = tmp_r.next()
                act(tm, bks[:, :], AF.Sqrt, [bbs, b_ones], [bt], bias=eps_col, scale=1.0 / D)
                rr, brr = rr_r.next()
                recip(rr, tm, [bt], [brr])
                for c in range(16):
                    stt("dve", xa[:, c, :], xa[:, c, :], wf[:, c:c + 1], rr, ALU.mult, ALU.mult,
                        [bxa, b_nrm, brr], [bxa])
                for j in range(4):
                    tti = blk * 4 + j
                    yo, byo = yo_r.next()
                    for g in range(4):
                        bk, bb = banks[k % 4], pb[k % 4]
                        k += 1
                        for jj in range(4):
                            c = g * 4 + jj
                            tr(bk[:, jj * 128:(jj + 1) * 128], xa[:, c, j * 128:(j + 1) * 128], ident, [bxa, b_cst], [bb])
                        cp("act" if g % 2 else "dve", yo[:, g * 512:(g + 1) * 512], bk[:, :], [bb], [byo])
                    final_ops.append(P.dma("sp", y_out[tti * 128:(tti + 1) * 128, :], yo, [byo], [], byo))
            P.barrier()

        P.barrier()

        if "s0" in stages:
            stage0()
        for l in range(nlayers):
            if "A" in stages:
                stageA(l)
            if "B1" in stages:
                stageB1(l)
            if "B2" in stages:
                stageB2(l)
            if "B3" in stages:
                stageB3(l)
            if "C" in stages:
                stageC(l)
            if "D" in stages:
                stageD(l)
        if "F" in stages:
            stageF()
        P.emit(final_ops)
    return nc


def prep_inputs(inp):
    f = lambda a: np.ascontiguousarray(np.asarray(a, dtype=np.float32))
    xp, xs = f(inp["x_prompt"]), f(inp["x_sample"])
    shared = {
        "w_in": f(inp["w_in"]), "w_out": f(inp["w_out"]), "w_up": f(inp["ffn_w_up"]), "w_dn": f(inp["ffn_w_down"]),
    }
    n1, n2, nf = f(inp["norm1_w"]), f(inp["norm2_w"]), f(inp["final_norm_w"])
    nrm = np.stack([n1[0], n1[1], n2[0], n2[1], nf], 0).reshape(5, 16, 128).transpose(2, 0, 1)
    shared["nrm"] = np.ascontiguousarray(nrm)
    cw, cb = f(inp["ffn_conv_w"]), f(inp["ffn_conv_b"])
    cwb = np.concatenate([cw, cb[:, None, :]], 1)
    shared["cwb"] = np.ascontiguousarray(cwb.reshape(2, 4, 44, 128).transpose(0, 3, 2, 1))
    df, db = f(inp["ret_decay_fwd"]), f(inp["ret_decay_bwd"])
    rdec = np.zeros((2, 128, 24), np.float32)
    hp = np.arange(128) // 64
    for l in range(2):
        rdec[l, :, 0:8] = df[l][None, :]
        rdec[l, :, 8:16] = db[l][None, :]
        for pr in range(4):
            rdec[l, :, 16 + pr] = df[l][2 * pr + hp]
            rdec[l, :, 20 + pr] = db[l][2 * pr + hp]
    shared["rdec"] = rdec
    shared["rnw"] = np.ascontiguousarray(np.broadcast_to(f(inp["ret_norm_w"])[:, None, :], (2, 128, 512)))
    shared["dnw"] = np.ascontiguousarray(np.broadcast_to(f(inp["diff_norm_w"])[:, None, :], (2, 128, 768)))
    dl = np.concatenate([f(inp["diff_lambda_q1"]), f(inp["diff_lambda_k1"]), f(inp["diff_lambda_q2"]),
                         f(inp["diff_lambda_k2"])], 1)
    shared["dlam"] = np.ascontiguousarray(np.broadcast_to(dl[:, None, :], (2, 128, 256)))
    shared["tt"] = make_tt(f(inp["na_rpb"]))
    shared["cst"] = make_consts()
    import ml_dtypes
    aug = np.zeros((6, 4, 512), np.float32)
    ii = np.arange(512, dtype=np.float64)
    for h in range(6):
        v = (SLOPES[h] * ii).astype(np.float32)
        hi = v.astype(ml_dtypes.bfloat16).astype(np.float32)
        lo = (v - hi).astype(ml_dtypes.bfloat16).astype(np.float32)
        aug[h, 0], aug[h, 1], aug[h, 2], aug[h, 3] = hi, lo, hi, lo
    shared["augq"] = aug
    in_maps = []
    for c in range(8):
        m = dict(shared)
        if c < 4:
            m["x"] = np.ascontiguousarray(xp[2 * c:2 * c + 2].reshape(T, D))
            m["flg"] = make_flags(0)
        else:
            m["x"] = np.ascontiguousarray(xs[c - 4].reshape(T, D))
            m["flg"] = make_flags(1)
        in_maps.append(m)
    return in_maps


_NC = None


def kernel(**inputs):
    global _NC
    in_maps = prep_inputs(inputs)
    if _NC is None:
        _NC = build()
    res = run_bass_kernel_spmd(_NC, in_maps, core_ids=list(range(8)))
    ys = [np.asarray(r["y"], dtype=np.float32) for r in res.results]
    y_prompt = np.stack([ys[c].reshape(2, 2048, D) for c in range(4)], 0).reshape(8, 2048, D)
    y_sample = np.stack([ys[c].reshape(4096, D) for c in range(4, 8)], 0)
    return (y_prompt, y_sample)
```

```python
import math
import os
import contextlib
import numpy as np
import concourse.bass as bass
import concourse.mybir as mybir
from concourse.bass_utils import run_bass_kernel_spmd

F32 = mybir.dt.float32
BF16 = mybir.dt.bfloat16
AF = mybir.ActivationFunctionType
ALU = mybir.AluOpType
AX = mybir.AxisListType

ENGS = ("pe", "dve", "act", "pool", "sp")
T = 4096
D = 2048
NEG = -30000.0
EPS = 1e-6


class SemSlot:
    __slots__ = ("count", "handle")

    def __init__(self):
        self.count = 0
        self.handle = None


class Buf:
    __slots__ = ("name", "writers", "readers", "slot", "gen_deps")

    def __init__(self, name="b"):
        self.name = name
        self.writers = []
        self.readers = []
        self.slot = None
        self.gen_deps = []


class Op:
    __slots__ = ("eng", "fn", "deps", "is_dma", "slot", "target", "needed")

    def __init__(self, eng, fn, is_dma):
        self.eng = eng
        self.fn = fn
        self.deps = []
        self.is_dma = is_dma
        self.slot = None
        self.target = None
        self.needed = False


class Prog:
    def __init__(self, nc):
        self.nc = nc
        self.ops = {e: [] for e in ENGS}
        self.slots = []
        self.free_slots = []
        self.used_slots = []
        self.bar = {e: [] for e in ENGS}
        self.dma_since_bar = []

    def _dep(self, op, prod):
        if prod is op:
            return
        if prod.eng == "pe" and op.eng == "pe" and not prod.is_dma and not op.is_dma:
            return
        op.deps.append(prod)

    def op(self, eng, fn, reads=(), writes=(), dma_key=None):
        is_dma = dma_key is not None
        o = Op(eng, fn, is_dma)
        if self.bar[eng]:
            o.deps.extend(self.bar[eng])
            self.bar[eng] = []
        for b in reads:
            for w in b.writers:
                self._dep(o, w)
        for b in writes:
            if is_dma and b.writers and not b.readers and all(w.is_dma for w in b.writers):
                for d_ in b.gen_deps:
                    self._dep(o, d_)
                b.writers.append(o)
                continue
            b.gen_deps = list(b.readers) + list(b.writers)
            for r in b.readers:
                self._dep(o, r)
            for w in b.writers:
                self._dep(o, w)
            b.writers = [o]
            b.readers = []
        for b in reads:
            if o not in b.writers:
                b.readers.append(o)
        if is_dma:
            if dma_key.slot is None:
                if self.free_slots:
                    dma_key.slot = self.free_slots.pop()
                else:
                    dma_key.slot = SemSlot()
                    self.slots.append(dma_key.slot)
                self.used_slots.append((dma_key, dma_key.slot))
            s = dma_key.slot
            s.count += 16
            o.slot = s
            o.target = s.count
            self.dma_since_bar.append(o)
        self.ops[eng].append(o)
        return o

    def barrier(self):
        deps = list(self.dma_since_bar)
        for e in ENGS:
            for o in reversed(self.ops[e]):
                if not o.is_dma:
                    deps.append(o)
                    break
        for e in ENGS:
            self.bar[e] = list(deps)
        self.dma_since_bar = []
        for b, s in self.used_slots:
            b.slot = None
            self.free_slots.append(s)
        self.used_slots = []

    def dma(self, q, out, in_, reads, writes, key):
        return self.op(q, lambda e: e.dma_start(out=out, in_=in_), reads, writes, dma_key=key)

    def emit(self, final_ops=()):
        nc = self.nc
        for e in ENGS:
            for o in self.ops[e]:
                for d in o.deps:
                    if not d.is_dma:
                        d.needed = True
        cnt = {e: 0 for e in ENGS}
        for e in ENGS:
            for o in self.ops[e]:
                if not o.is_dma and o.needed:
                    cnt[e] += 1
                    o.target = cnt[e]
        with contextlib.ExitStack() as st:
            esem = {e: st.enter_context(nc.semaphore("s_" + e)) for e in ENGS}
            for i, s in enumerate(self.slots):
                s.handle = st.enter_context(nc.semaphore("d%d" % i))
            block = st.enter_context(nc.Block())
            prog = self

            def run(eng_name, eng):
                waited = {}
                for o in prog.ops[eng_name]:
                    need = {}
                    for d in o.deps:
                        s = d.slot.handle if d.is_dma else esem[d.eng]
                        k = id(s)
                        if waited.get(k, 0) >= d.target:
                            continue
                        if k not in need or need[k][1] < d.target:
                            need[k] = (s, d.target)
                    for k, (s, v) in need.items():
                        eng.wait_ge(s, v)
                        waited[k] = v
                    ins = o.fn(eng)
                    if o.is_dma:
                        ins.then_inc(o.slot.handle, 16)
                    elif o.needed:
                        ins.then_inc(esem[eng_name], 1)
                if eng_name == "sp":
                    fin = {}
                    for o in final_ops:
                        k = id(o.slot)
                        if k not in fin or fin[k][1] < o.target:
                            fin[k] = (o.slot.handle, o.target)
                    for k, (s, v) in fin.items():
                        eng.wait_ge(s, v)
                    for sl_ in prog.slots:
                        if sl_.count > 0:
                            eng.wait_ge(sl_.handle, sl_.count)

            @block.tensor
            def _(e):
                run("pe", e)

            @block.vector
            def _(e):
                run("dve", e)

            @block.scalar
            def _(e):
                run("act", e)

            @block.gpsimd
            def _(e):
                run("pool", e)

            @block.sync
            def _(e):
                run("sp", e)


class Arena:
    def __init__(self, tensor, nbytes):
        self.t = tensor
        self.cap = nbytes
        self.off = 0
        self.base = 0

    def mark(self):
        self.base = self.off

    def reset(self):
        if os.environ.get("ARENA_DBG"):
            print("arena high-water", getattr(self, "hw", 0))
        self.hw = 0
        self.off = self.base

    def alloc(self, shape, dt):
        n = 1
        for s in shape:
            n *= s
        nb = n * (4 if dt == F32 else 2)
        off = (self.off + 63) // 64 * 64
        self.off = off + nb
        assert self.off <= self.cap, ("arena overflow", self.off, self.cap)
        self.hw = max(getattr(self, "hw", 0), self.off)
        v = self.t[:, off // 2:(off + nb) // 2]
        if dt == F32:
            v = v.bitcast(F32)
        if len(shape) == 2:
            v = v.rearrange("p (a b) -> p a b", b=shape[1])
        elif len(shape) == 3:
            v = v.rearrange("p (a b c) -> p a b c", b=shape[1], c=shape[2])
        return v


class Ring:
    def __init__(self, arena, n, shape, dt, name):
        self.items = [(arena.alloc(shape, dt), Buf("%s%d" % (name, i))) for i in range(n)]
        self.i = 0

    def next(self):
        it = self.items[self.i % len(self.items)]
        self.i += 1
        return it


C_ID = 0
C_NIDXF = 128
C_NIDXB = 256
C_NPOS1 = 384
C_NPOSB = 512
C_NK1 = 640
C_NK0 = 641
C_ONE = 642
C_KAUG = 643
C_IOTA = 644
C_NEAR = 1156
C_BCA = 3204
C_BCB = 3396
CW = 3588

SLOPES = [2.0 ** (-8.0 * (h + 1.0) / 6.0) for h in range(6)]


def make_consts():
    c = np.zeros((128, CW), np.float32)
    p = np.arange(128)[:, None].astype(np.float64)
    i = np.arange(128)[None, :].astype(np.float64)
    c[:, C_ID:C_ID + 128] = np.eye(128)
    c[:, C_NIDXF:C_NIDXF + 128] = np.where(i >= p, -(i - p), -1e6)
    c[:, C_NIDXB:C_NIDXB + 128] = np.where(p > i, -(p - i), -1e6)
    c[:, C_NPOS1:C_NPOS1 + 128] = -(i + 1)
    c[:, C_NPOSB:C_NPOSB + 128] = -(128 - i)
    c[:, C_NK1] = -(127 - p[:, 0])
    c[:, C_NK0] = -p[:, 0]
    c[:, C_ONE] = 1.0
    c[64:66, C_KAUG] = 1.0
    c[66:68, C_KAUG] = -2.0
    ii = np.arange(512)[None, :].astype(np.float64)
    c[:, C_IOTA:C_IOTA + 512] = ii
    for m in range(4):
        c[:, C_NEAR + 512 * m:C_NEAR + 512 * (m + 1)] = -np.abs(ii - 128 * m - p)
    for h in range(6):
        for d in range(32):
            c[:, C_BCA + h * 32 + d] = -SLOPES[h] * (128 * d + p[:, 0])
            c[:, C_BCB + h * 32 + d] = -SLOPES[h] * (128 * d - p[:, 0])
    return c


def na_rs(kind, rq):
    if kind == 0:
        s, r = divmod(rq, 32)
        return 32 * s + min(max(r - 4, 0), 24)
    return min(max(rq - 4, 0), 56)


def na_plan():
    plan = []
    slot = 0
    for rq in range(64):
        a, b = na_rs(0, rq), na_rs(1, rq)
        if a == b:
            plan.append((a, 4, None))
        else:
            lo = min(a, b)
            hi = max(a, b) + 8
            nk = (hi - lo + 1) // 2
            plan.append((lo, nk, slot))
            slot += nk
    return plan, slot


NA_PLAN, NA_NSLOT = na_plan()
FLG_W = 2 + NA_NSLOT


def make_flags(kind):
    f = np.zeros((128, FLG_W), np.float32)
    f[:, 0] = 1.0 if kind == 1 else 0.0
    f[:, 1] = 0.0 if kind == 1 else NEG
    for rq, (ws, nk, slot) in enumerate(NA_PLAN):
        if slot is None:
            continue
        rs = na_rs(kind, rq)
        for m in range(nk):
            for half in range(2):
                rk = ws + 2 * m + half
                ok = rs <= rk < rs + 8
                f[64 * half:64 * half + 64, 2 + slot + m] = 0.0 if ok else NEG
    return f


def make_tt(rpb):
    ck = np.arange(64)[:, None]
    cq = np.arange(64)[None, :]
    cs = np.clip(cq - 8, 0, 48)
    inwin = (ck >= cs) & (ck < cs + 16)
    dc = np.clip(ck - cq + 15, 0, 30)
    out = np.full((2, 12, 128, 15, 64), NEG, np.float32)
    for half in range(2):
        for ap in range(15):
            a = ap + half
            if a > 14:
                continue
            g = rpb[:, :, a, :][:, :, dc]
            out[:, :, 64 * half:64 * half + 64, ap, :] = np.where(inwin[None, None], g, np.float32(NEG))
    out = out.reshape(2, 6, 2, 128, 15, 64).transpose(0, 1, 3, 2, 4, 5)
    return np.ascontiguousarray(out.reshape(2, 6, 128, 2 * 15 * 64))


FM_GROUPS = [(0, 4, 0, 1.0), (512, 4, 512, 0.125), (2048, 6, 1024, 0.125), (2816, 6, 1792, 1.0),
             (4352, 6, 2560, 0.125), (5120, 6, 3328, 1.0)]
TM_SLABS = [(512, 0, True), (1024, 512, False), (1536, 1024, False), (3584, 1536, False),
            (4096, 2048, False), (6144, 2560, False)]


def build(stages=("s0", "A", "B1", "B2", "B3", "C", "D", "F"), nlayers=2, dbg=()):
    nc = bass.Bass("TRN2", target_bir_lowering=False)

    def din(name, shape, dt=F32):
        return nc.dram_tensor(name, shape, dt, kind="ExternalInput").ap()

    x_in = din("x", [T, D])
    w_in = din("w_in", [2, D, 6656])
    w_out = din("w_out", [2, D, D])
    w_up = din("w_up", [2, D, 11264])
    w_dn = din("w_dn", [2, 5632, D])
    nrm = din("nrm", [128, 5, 16])
    cwb = din("cwb", [2, 128, 44, 4])
    rdec = din("rdec", [2, 128, 24])
    rnw = din("rnw", [2, 128, 512])
    dnw = din("dnw", [2, 128, 768])
    dlam = din("dlam", [2, 128, 256])
    tt_in = din("tt", [2, 6, 128, 1920])
    cst = din("cst", [128, CW])
    augq = din("augq", [6, 4, 512])
    flg = din("flg", [128, FLG_W])
    y_out = nc.dram_tensor("y", [T, D], F32, kind="ExternalOutput").ap()

    def dscr(name, shape, dt):
        if name in dbg:
            return nc.dram_tensor(name, shape, dt, kind="ExternalOutput").ap()
        return nc.dram_tensor(name, shape, dt).ap()

    xT = dscr("xT", [D, T], F32)
    qkT = dscr("qkT", [4096, T], BF16)
    tokm = dscr("tokm", [T, 3072], BF16)
    mixT = dscr("mixT", [D, T], BF16)
    h2T = dscr("h2T", [D, T + 64], BF16)
    rs2 = dscr("rs2", [128, T + 32], F32)

    P = Prog(nc)
    final_ops = []
    with contextlib.ExitStack() as st:
        ARENA_BYTES = 186 * 1024
        arena_t = st.enter_context(nc.sbuf_tensor("arena", [128, ARENA_BYTES // 2], BF16))
        A = Arena(arena_t, ARENA_BYTES)
        psum_all = st.enter_context(nc.psum_tensor("psum_all", [128, 4096], F32))
        banks = [psum_all[:, i * 512:(i + 1) * 512] for i in range(8)]

        def PB(name="ps"):
            return [Buf("%s%d" % (name, i)) for i in range(8)]

        def mm(out, lhsT, rhs, start, stop, reads, writes):
            return P.op("pe", lambda e: e.matmul(out, lhsT=lhsT, rhs=rhs, start=start, stop=stop), reads, writes)

        def tr(out, in_, ident, reads, writes):
            return P.op("pe", lambda e: e.transpose(out, in_, ident), reads, writes)

        def act(out, in_, func, reads, writes, bias=None, scale=None, accum=None, eng="act"):
            kw = {}
            if bias is not None:
                kw["bias"] = bias
            if scale is not None:
                kw["scale"] = scale
            if accum is not None:
                kw["accum_out"] = accum
            return P.op(eng, lambda e: e.activation(out=out, in_=in_, func=func, **kw), reads, writes)

        def amul(out, in_, m, reads, writes):
            return P.op("act", lambda e: e.mul(out, in_, m), reads, writes)

        def tt(eng, out, in0, in1, op, reads, writes):
            return P.op(eng, lambda e: e.tensor_tensor(out=out, in0=in0, in1=in1, op=op), reads, writes)

        def ts(eng, out, in0, s1, s2, op0, op1, reads, writes):
            if op1 is None:
                return P.op(eng, lambda e: e.tensor_scalar(out=out, in0=in0, scalar1=s1, scalar2=None, op0=op0), reads, writes)
            return P.op(eng, lambda e: e.tensor_scalar(out=out, in0=in0, scalar1=s1, scalar2=s2, op0=op0, op1=op1), reads, writes)

        def stt(eng, out, in0, scalar, in1, op0, op1, reads, writes):
            return P.op(eng, lambda e: e.scalar_tensor_tensor(out=out, in0=in0, scalar=scalar, in1=in1, op0=op0, op1=op1), reads, writes)

        def cp(eng, out, in_, reads, writes):
            if eng == "act":
                return P.op("act", lambda e: e.copy(out, in_), reads, writes)
            return P.op(eng, lambda e: e.tensor_copy(out, in_), reads, writes)

        def rsum(eng, out, in_, reads, writes):
            return P.op(eng, lambda e: e.reduce_sum(out=out, in_=in_, axis=AX.X), reads, writes)

        def recip(out, in_, reads, writes):
            return P.op("dve", lambda e: e.reciprocal(out=out, in_=in_), reads, writes)

        def mset(eng, ap, val, writes):
            return P.op(eng, lambda e: e.memset(ap, val), [], writes)

        def load_cast(stg_r, q, cast_eng, dst, dbuf, src):
            a_, b_ = src.shape[1], src.shape[2]
            stg, bst = stg_r.next()
            view = stg[:, 0:a_ * b_].rearrange("p (a b) -> p a b", b=b_)
            P.dma(q, view, src, [], [bst], bst)
            cp(cast_eng, dst, view, [bst], [dbuf])

        cst_sb = A.alloc([C_IOTA], F32)
        b_cst = Buf("cst")
        P.dma("sp", cst_sb, cst[:, 0:C_IOTA], [], [b_cst], b_cst)
        flg_sb = A.alloc([FLG_W], F32)
        b_flg = Buf("flg")
        P.dma("sp", flg_sb, flg, [], [b_flg], b_flg)
        nrm_sb = A.alloc([5, 16], F32)
        b_nrm = Buf("nrm")
        P.dma("sp", nrm_sb, nrm, [], [b_nrm], b_nrm)
        ident = cst_sb[:, C_ID:C_ID + 128]
        ident_bf = A.alloc([128], BF16)
        b_idb = Buf("identbf")
        cp("dve", ident_bf, ident, [b_cst], [b_idb])
        keep_col = flg_sb[:, 0:1]
        xneg_col = flg_sb[:, 1:2]
        A.mark()

        def stage0():
            A.reset()
            pb = PB("s0ps")
            xin = Ring(A, 2, [D], F32, "xin")
            xst = Ring(A, 2, [16, 128], F32, "xst")
            xT_v = xT.rearrange("(c p) t -> p c t", p=128)
            k = 0
            for tti in range(32):
                xt, bx = xin.next()
                P.dma("sp", xt, x_in[tti * 128:(tti + 1) * 128, :], [], [bx], bx)
                so, bs = xst.next()
                for g in range(4):
                    bk = banks[k % 8]
                    bb = pb[k % 8]
                    k += 1
                    for j in range(4):
                        c = g * 4 + j
                        tr(bk[:, j * 128:(j + 1) * 128], xt[:, c * 128:(c + 1) * 128], ident, [bx, b_cst], [bb])
                    cp("act" if g % 2 else "dve", so[:, g * 4:(g + 1) * 4, :],
                       bk[:, :].rearrange("p (a b) -> p a b", b=128), [bb], [bs])
                P.dma("pool", xT_v[:, :, tti * 128:(tti + 1) * 128], so, [bs], [], bs)
            P.barrier()

        def stageA(l):
            w_v = w_in[l].rearrange("(c p) n -> p c n", p=128)
            w1 = nrm_sb[:, l, :]
            chunks = []
            for (wo, n, ro, sc) in FM_GROUPS:
                for j in range(n):
                    chunks.append((wo + j * 128, ro + j * 128, sc))
            slabs = [(512, 512, 0, True), (1024, 512, 512, False), (1536, 512, 1024, False),
                     (3584, 512, 1536, False), (4096, 256, 2048, False), (5888, 256, 2304, False),
                     (6144, 512, 2560, False)]
            for half in range(2):
                A.reset()
                pb = PB("Aps")
                TH = 2048
                tb = half * TH
                hT = A.alloc([16, TH], BF16)
                b_hT = [Buf("hT%d" % i) for i in range(4)]
                rrow = A.alloc([TH], F32)
                b_rrow = [Buf("rrow%d" % i) for i in range(4)]
                rcol = A.alloc([16], F32)
                rcolk = A.alloc([16], F32)
                b_rcol = Buf("rcol")
                xs_r = Ring(A, 4, [512], F32, "xs")
                sq_r = Ring(A, 3, [512], BF16, "sq")
                tmp_r = Ring(A, 2, [512], F32, "tmpA")
                for blk in range(4):
                    sl = slice(blk * 512, (blk + 1) * 512)
                    gsl = slice(tb + blk * 512, tb + (blk + 1) * 512)
                    bk, bb = banks[blk % 2], pb[blk % 2]
                    for c in range(16):
                        xs, bx = xs_r.next()
                        P.dma("sp", xs, xT[c * 128:(c + 1) * 128, gsl], [], [bx], bx)
                        sq, bs = sq_r.next()
                        act(sq, xs, AF.Square, [bx], [bs])
                        mm(bk[:, :], ones128, sq, c == 0, c == 15, [bs, b_ones], [bb])
                        ts("dve", hT[:, c, sl], xs, w1[:, c:c + 1], None, ALU.mult, None, [bx, b_nrm], [b_hT[blk]])
                    tm, bt = tmp_r.next()
                    act(tm, bk[:, :], AF.Sqrt, [bb, b_ones], [bt], bias=eps_col, scale=1.0 / D)
                    recip(rrow[:, sl], tm, [bt], [b_rrow[blk]])
                for tti in range(16):
                    bk, bb = banks[2 + tti % 2], pb[2 + tti % 2]
                    tr(bk[:, 0:128], rrow[:, tti * 128:(tti + 1) * 128], ident, [b_rrow[tti // 4], b_cst], [bb])
                    cp("dve", rcol[:, tti:tti + 1], bk[:, 0:1], [bb], [b_rcol])
                ts("dve", rcolk, rcol, 0.125, None, ALU.mult, None, [b_rcol], [b_rcol])
                wc_r = Ring(A, 3, [16, 128], BF16, "wc")
                stg_r = Ring(A, 4, [2048], F32, "stgA")
                st_r = Ring(A, 2, [TH], BF16, "stA2")
                ws_r = Ring(A, 2, [16, 512], BF16, "ws")
                st3_r = Ring(A, 3, [512], BF16, "stA3")
                work = [("fm",) + ch for ch in chunks] + [("tm",) + sl_ for sl_ in slabs]
                kk = [0]

                def a_load(item):
                    if item[0] == "fm":
                        wc, bw = wc_r.next()
                        load_cast(stg_r, "sp", "act", wc, bw, w_v[:, :, item[1]:item[1] + 128])
                        return (wc, bw)
                    _, wo, wn, to, ks = item
                    wsb, bw = ws_r.next()
                    for pc in range(wn // 128):
                        load_cast(stg_r, "sp", "act", wsb[:, :, pc * 128:(pc + 1) * 128], bw,
                                  w_v[:, :, wo + pc * 128:wo + (pc + 1) * 128])
                    return (wsb, bw)

                def a_compute(item, wt):
                    if item[0] == "fm":
                        _, wo, ro, sc = item
                        wc, bw = wt
                        so, bs = st_r.next()
                        for blk in range(4):
                            sl = slice(blk * 512, (blk + 1) * 512)
                            bk, bb = banks[4 + kk[0] % 4], pb[4 + kk[0] % 4]
                            kk[0] += 1
                            for c in range(16):
                                mm(bk[:, :], wc[:, c, :], hT[:, c, sl], c == 0, c == 15, [bw, b_hT[blk]], [bb])
                            stt("dve", so[:, sl], bk[:, :], sc, rrow[:, sl], ALU.mult, ALU.mult, [bb, b_rrow[blk]], [bs])
                        P.dma("pool", qkT[ro:ro + 128, tb:tb + TH], so, [bs], [], bs)
                        return
                    _, wo, wn, to, ks = item
                    wsb, bw = wt
                    for tti in range(16):
                        bk, bb = banks[kk[0] % 4], pb[kk[0] % 4]
                        kk[0] += 1
                        for c in range(16):
                            mm(bk[:, 0:wn], hT[:, c, tti * 128:(tti + 1) * 128], wsb[:, c, 0:wn], c == 0, c == 15,
                               [bw, b_hT[tti // 4]], [bb])
                        so, bs = st3_r.next()
                        amul(so[:, 0:wn], bk[:, 0:wn], (rcolk if ks else rcol)[:, tti:tti + 1], [bb, b_rcol], [bs])
                        g0 = tb + tti * 128
                        P.dma("pool", tokm[g0:g0 + 128, to:to + wn], so[:, 0:wn], [bs], [], bs)

                DEPTH = 1
                wts = []
                for i in range(len(work) + DEPTH):
                    if i < len(work):
                        wts.append(a_load(work[i]))
                    if i >= DEPTH:
                        a_compute(work[i - DEPTH], wts[i - DEPTH])
                P.barrier()

        ones128 = A.alloc([128], BF16)
        b_ones = Buf("ones")
        mset("dve", ones128, 1.0, [b_ones])
        eps_col = A.alloc([1], F32)
        mset("dve", eps_col, EPS, [b_ones])
        A.mark()

        def stageB1(l):
            A.reset()
            pb = PB("B1ps")
            dec = A.alloc([24], F32)
            b_dec = Buf("dec")
            P.dma("sp", dec, rdec[l], [], [b_dec], b_dec)
            nlg = A.alloc([24], F32)
            tA = A.alloc([24], F32)
            tB = A.alloc([24], F32)
            b_c = Buf("B1c")
            act(tA, dec, AF.Exp, [b_dec], [b_c], scale=-1.0)
            ts("dve", tB, tA, -0.2, 0.25, ALU.mult, ALU.add, [b_c], [b_c])
            tt("dve", tB, tB, tA, ALU.mult, [b_c], [b_c])
            ts("dve", tB, tB, -1.0, 1.0 / 3.0, ALU.mult, ALU.add, [b_c], [b_c])
            tt("dve", tB, tB, tA, ALU.mult, [b_c], [b_c])
            ts("dve", tB, tB, -1.0, 0.5, ALU.mult, ALU.add, [b_c], [b_c])
            tt("dve", tB, tB, tA, ALU.mult, [b_c], [b_c])
            ts("dve", tB, tB, -1.0, 1.0, ALU.mult, ALU.add, [b_c], [b_c])
            tt("dve", nlg, tB, tA, ALU.mult, [b_c], [b_c])
            nlgf_b, nlgb_b = nlg[:, 0:8], nlg[:, 8:16]
            nlgf_p, nlgb_p = nlg[:, 16:20], nlg[:, 20:24]
            DT = A.alloc([8, 128], F32)
            e2 = A.alloc([128], F32)
            for h in range(8):
                act(DT[:, h, :], cst_sb[:, C_NIDXF:C_NIDXF + 128], AF.Exp, [b_c, b_cst], [b_c], scale=nlgf_b[:, h:h + 1])
                act(e2, cst_sb[:, C_NIDXB:C_NIDXB + 128], AF.Exp, [b_c, b_cst], [b_c], scale=nlgb_b[:, h:h + 1])
                tt("dve", DT[:, h, :], DT[:, h, :], e2, ALU.add, [b_c], [b_c])
            QDF = A.alloc([4, 128], F32)
            QDB = A.alloc([4, 128], F32)
            for pr in range(4):
                act(QDF[:, pr, :], cst_sb[:, C_NPOS1:C_NPOS1 + 128], AF.Exp, [b_c, b_cst], [b_c], scale=nlgf_p[:, pr:pr + 1])
                act(QDB[:, pr, :], cst_sb[:, C_NPOSB:C_NPOSB + 128], AF.Exp, [b_c, b_cst], [b_c], scale=nlgb_p[:, pr:pr + 1])
            KDF = A.alloc([8], F32)
            KDB = A.alloc([8], F32)
            act(KDF, nlgf_b, AF.Exp, [b_c, b_cst], [b_c], scale=cst_sb[:, C_NK1:C_NK1 + 1])
            act(KDB, nlgb_b, AF.Exp, [b_c, b_cst], [b_c], scale=cst_sb[:, C_NK0:C_NK0 + 1])
            CD = A.alloc([8], F32)
            act(CD, nlg[:, 16:24], AF.Exp, [b_c], [b_c], scale=-128.0)
            CDFB = A.alloc([4, 64], F32)
            CDBB = A.alloc([4, 64], F32)
            cp("dve", CDFB, CD[:, 0:4].unsqueeze(2).to_broadcast([128, 4, 64]), [b_c], [b_c])
            cp("dve", CDBB, CD[:, 4:8].unsqueeze(2).to_broadcast([128, 4, 64]), [b_c], [b_c])
            nw = A.alloc([512], F32)
            b_nw = Buf("rnw")
            P.dma("sp", nw, rnw[l], [], [b_nw], b_nw)
            LV = int(os.environ.get("B1LV", "9"))
            qt_r = Ring(A, 3, [2, 4, 128], BF16, "QKT")
            qk_v = qkT[0:1024, :].rearrange("(g c p) t -> p g c t", p=128, g=2)
            tk_r = Ring(A, 4, [1536], BF16, "TK")
            tk_v = tokm[:, 0:1536].rearrange("(c p) f -> p c f", p=128)
            Sf_all = A.alloc([32, 256], BF16)
            Sb_all = A.alloc([32, 256], BF16)
            Ub_all = A.alloc([32, 256], F32)
            b_Sf = [Buf("Sf%d" % i) for i in range(32)]
            b_Sb = [Buf("Sb%d" % i) for i in range(32)]
            b_Ub = [Buf("Ub%d" % i) for i in range(32)]
            S = A.alloc([256], F32)
            b_S = Buf("S")
            kd_r = Ring(A, 2, [2, 8, 64], BF16, "kd")
            tmpS = A.alloc([256], F32)
            mset("dve", S, 0.0, [b_S])
            for c in range(32):
                kd, bkd = kd_r.next()
                tkc, btk = tk_r.next()
                P.dma("sp", tkc[:, 0:1024], tk_v[:, c, 0:1024], [], [btk], btk)
                rk_c = tkc[:, 0:512].rearrange("p (h d) -> p h d", d=64)
                rv_c = tkc[:, 512:1024].rearrange("p (h d) -> p h d", d=64)
                tt("pool", kd[:, 0, :, :], rk_c, KDF.unsqueeze(2).to_broadcast([128, 8, 64]), ALU.mult, [btk, b_c], [bkd])
                tt("pool", kd[:, 1, :, :], rk_c, KDB.unsqueeze(2).to_broadcast([128, 8, 64]), ALU.mult, [btk, b_c], [bkd])
                bkf, bbf = banks[(2 * c) % 4], pb[(2 * c) % 4]
                bkb, bbb = banks[(2 * c + 1) % 4], pb[(2 * c + 1) % 4]
                for h in range(8):
                    po = (h % 2) * 64
                    mm(bkf[po:po + 64, (h // 2) * 64:(h // 2) * 64 + 64], kd[:, 0, h, :], rv_c[:, h, :], True, True,
                       [bkd, btk], [bbf])
                    mm(bkb[po:po + 64, (h // 2) * 64:(h // 2) * 64 + 64], kd[:, 1, h, :], rv_c[:, h, :], True, True,
                       [bkd, btk], [bbb])
                cp("act", Ub_all[:, c, :], bkb[:, 0:256], [bbb], [b_Ub[c]])
                if c == 16:
                    ts("dve", S, S, keep_col, None, ALU.mult, None, [b_S, b_flg], [b_S])
                cp("dve", Sf_all[:, c, :], S, [b_S], [b_Sf[c]])
                tt("dve", tmpS, S, CDFB.rearrange("p a b -> p (a b)"), ALU.mult, [b_S, b_c], [b_S])
                tt("dve", S, tmpS, bkf[:, 0:256], ALU.add, [b_S, bbf], [b_S])
            Sb = A.alloc([256], F32)
            b_Sbr = Buf("Sbr")
            mset("dve", Sb, 0.0, [b_Sbr])
            for c in range(31, -1, -1):
                if c == 15:
                    ts("dve", Sb, Sb, keep_col, None, ALU.mult, None, [b_Sbr, b_flg], [b_Sbr])
                cp("dve", Sb_all[:, c, :], Sb, [b_Sbr], [b_Sb[c]])
                tt("dve", tmpS, Sb, CDBB.rearrange("p a b -> p (a b)"), ALU.mult, [b_Sbr, b_c], [b_Sbr])
                tt("dve", Sb, tmpS, Ub_all[:, c, :], ALU.add, [b_Sbr, b_Ub[c]], [b_Sbr])
            MT_r = Ring(A, 2, [8, 128], BF16, "MT")
            qd_r = Ring(A, 2, [2, 4, 128], BF16, "qd")
            o_r = Ring(A, 2, [8, 64], F32, "oret")
            sg_r = Ring(A, 2, [512], F32, "sg")
            st4 = Ring(A, 2, [8, 4], F32, "stat")
            yb_r = Ring(A, 2, [512], BF16, "yb")
            ot_r = Ring(A, 3, [4, 128], BF16, "outT")
            mix_ret_v = mixT[0:512, :].rearrange("(c p) t -> p c t", p=128)
            st2 = {}

            def P1(c):
                gcs = slice(c * 128, (c + 1) * 128)
                qkc, b_QT = qt_r.next()
                P.dma("sp", qkc, qk_v[:, :, :, gcs], [], [b_QT], b_QT)
                QT = qkc[:, 0, :, :]
                KT = qkc[:, 1, :, :]
                MT, bMT = MT_r.next()
                for hh in range(4):
                    for par in range(2):
                        po = par * 64
                        mm(banks[4 + par][:, hh * 128:(hh + 1) * 128], KT[po:po + 64, hh, :], QT[po:po + 64, hh, :], True, True,
                           [b_QT], [pb[4 + par]])
                for par in range(2):
                    tt("dve", MT[:, par:8:2, :], banks[4 + par][:, :].rearrange("p (a b) -> p a b", b=128),
                       DT[:, par:8:2, :], ALU.mult, [pb[4 + par], b_c], [bMT])
                qd, bqd = qd_r.next()
                tt("pool", qd[:, 0, :, :], QT, QDF, ALU.mult, [b_QT, b_c], [bqd])
                tt("pool", qd[:, 1, :, :], QT, QDB, ALU.mult, [b_QT, b_c], [bqd])
                st2[c] = (MT, bMT, qd, bqd)

            def P2(c):
                MT, bMT, qd, bqd = st2.pop(c)
                bk, bb = banks[6 + c % 2], pb[6 + c % 2]
                tkc, btk = tk_r.next()
                P.dma("sp", tkc[:, 512:1536], tk_v[:, c, 512:1536], [], [btk], btk)
                rv_c = tkc[:, 512:1024].rearrange("p (h d) -> p h d", d=64)
                for h in range(8):
                    po = (h % 2) * 64
                    pr = h // 2
                    o_ps = bk[:, h * 64:(h + 1) * 64]
                    mm(o_ps, MT[:, h, :], rv_c[:, h, :], True, False, [bMT, btk], [bb])
                    mm(o_ps, qd[po:po + 64, 0, pr, :], Sf_all[po:po + 64, c, pr * 64:(pr + 1) * 64], False, False,
                       [bqd, b_Sf[c]], [bb])
                    mm(o_ps, qd[po:po + 64, 1, pr, :], Sb_all[po:po + 64, c, pr * 64:(pr + 1) * 64], False, True,
                       [bqd, b_Sb[c]], [bb])
                o, bo = o_r.next()
                sv, bsv = st4.next()
                o_v = bk[:, :].rearrange("p (h d) -> p h d", d=64)
                rsum("dve", sv[:, :, 0], o_v, [bb], [bsv])
                ts("dve", sv[:, :, 1], sv[:, :, 0], -1.0 / 64.0, None, ALU.mult, None, [bsv], [bsv])
                tt("dve", o, o_v, sv[:, :, 1:2].to_broadcast([128, 8, 64]), ALU.add, [bb, bsv], [bo])
                sg, bsg = sg_r.next()
                tt("pool", sg.rearrange("p (h d) -> p h d", d=64), o, o, ALU.mult, [bo], [bsg])
                rsum("dve", sv[:, :, 2], sg.rearrange("p (h d) -> p h d", d=64), [bsg], [bsv])
                act(sv[:, :, 3], sv[:, :, 2], AF.Sqrt, [bsv, b_ones], [bsv], bias=eps_col, scale=1.0 / 64.0)
                recip(sv[:, :, 2], sv[:, :, 3], [bsv], [bsv])
                tt("dve", o, o, sv[:, :, 2:3].to_broadcast([128, 8, 64]), ALU.mult, [bo, bsv], [bo])
                act(sg, tkc[:, 1024:1536], AF.Silu, [btk, bsg], [bsg])
                tt("pool", sg, sg, nw, ALU.mult, [bsg, b_nw], [bsg])
                yb, byb = yb_r.next()
                tt("dve", yb, o.rearrange("p h d -> p (h d)"), sg, ALU.mult, [bo, bsg], [byb])
                st2[("y", c)] = (yb, byb)

            def P3(c):
                yb, byb = st2.pop(("y", c))
                bk2, bb2 = banks[c % 2], pb[c % 2]
                pbf = bk2[:, :].bitcast(BF16)
                for j in range(4):
                    tr(pbf[:, j * 128:(j + 1) * 128], yb[:, j * 128:(j + 1) * 128], ident_bf, [byb, b_idb], [bb2])
                ot, bot = ot_r.next()
                cp("act", ot, pbf[:, 0:512].rearrange("p (a b) -> p a b", b=128), [bb2], [bot])
                P.dma("sp", mix_ret_v[:, :, c * 128:(c + 1) * 128], ot, [bot], [], bot)

            P1(0)
            for c in range(32):
                if c + 1 < 32:
                    P1(c + 1)
                P2(c)
                if c >= 1:
                    P3(c - 1)
            P3(31)
            P.barrier()

        def stageB2(l):
            A.reset()
            pb = PB("B2ps")
            q_r = Ring(A, 2, [T], BF16, "naq")
            k_r = Ring(A, 2, [T], BF16, "nak")
            vr_r = Ring(A, 2, [33, 128], BF16, "navraw")
            va_r = Ring(A, 2, [32, 2, 65], BF16, "nava")
            vb_r = Ring(A, 2, [32, 2, 65], BF16, "navb")
            tt_r = Ring(A, 2, [2, 15, 64], F32, "natt")
            sb_r = Ring(A, 2, [2, 6, 64], F32, "nasb")
            pt_r = Ring(A, 3, [2, 6, 64], BF16, "napt")
            rc_r = Ring(A, 2, [2], F32, "narc")
            ob_r = Ring(A, 2, [128], BF16, "naob")
            ot_r = Ring(A, 2, [T], BF16, "naot")
            for (v, b) in va_r.items + vb_r.items:
                mset("pool", v, 1.0, [b])
            def load_pair(pair):
                qT, bq = q_r.next()
                kT, bk_ = k_r.next()
                P.dma("sp", qT, qkT[1024 + pair * 128:1024 + (pair + 1) * 128, :], [], [bq], bq)
                P.dma("sp", kT, qkT[1792 + pair * 128:1792 + (pair + 1) * 128, :], [], [bk_], bk_)
                vr, bvr = vr_r.next()
                c0 = 1536 + pair * 128
                P.dma("sp", vr[:, 0:32, :], tokm[:, c0:c0 + 128].rearrange("(c p) f -> p c f", p=128), [], [bvr], bvr)
                va, bva = va_r.next()
                cp("pool", va[:, :, :, 0:64], vr[:, 0:32, :].rearrange("p c (h d) -> p c h d", d=64), [bvr], [bva])
                vr2, bvr2 = vr_r.next()
                P.dma("sp", vr2[:, 0:31, :], tokm[64:64 + 31 * 128, c0:c0 + 128].rearrange("(c p) f -> p c f", p=128),
                      [], [bvr2], bvr2)
                vb, bvb = vb_r.next()
                cp("pool", vb[:, 0:31, :, 0:64], vr2[:, 0:31, :].rearrange("p c (h d) -> p c h d", d=64), [bvr2], [bvb])
                ttb, btt = tt_r.next()
                P.dma("sp", ttb.rearrange("p a b c -> p (a b c)"), tt_in[l, pair], [], [btt], btt)
                oT, boT = ot_r.next()
                return dict(qT=qT, bq=bq, kT=kT, bk_=bk_, va=va, bva=bva, vb=vb, bvb=bvb, ttb=ttb, btt=btt, oT=oT, boT=boT)

            tiles = {0: load_pair(0)}
            steps = [(p_, rq_) for p_ in range(6) for rq_ in range(64)]
            pts = {}

            def score_banks(rq):
                kk = rq % 2
                return [(banks[2 * kk], pb[2 * kk]), (banks[2 * kk + 1], pb[2 * kk + 1])]

            def QK(j):
                p_, rq = steps[j]
                t_ = tiles[p_]
                ws, nk, slot = NA_PLAN[rq]
                for hh in range(2):
                    bk, bb = score_banks(rq)[hh]
                    for m in range(nk):
                        k0 = ws * 64 + 128 * m
                        mm(bk[:, m * 64:(m + 1) * 64], t_["kT"][hh * 64:hh * 64 + 64, k0:k0 + 128],
                           t_["qT"][hh * 64:hh * 64 + 64, rq * 64:(rq + 1) * 64], True, True, [t_["bk_"], t_["bq"]], [bb])

            def EW(j):
                p_, rq = steps[j]
                t_ = tiles[p_]
                ws, nk, slot = NA_PLAN[rq]
                sbt, bsb = sb_r.next()
                a0 = ws - rq + 7
                for hh in range(2):
                    bk, bb = score_banks(rq)[hh]
                    tt("dve", sbt[:, hh, 0:nk, :], bk[:, 0:nk * 64].rearrange("p (m q) -> p m q", q=64),
                       t_["ttb"][:, hh, a0:a0 + 2 * nk - 1:2, :], ALU.add, [bb, t_["btt"]], [bsb])
                pt, bpt = pt_r.next()
                if slot is None:
                    act(pt[:, :, 0:nk, :], sbt[:, :, 0:nk, :], AF.Exp, [bsb], [bpt])
                else:
                    for m in range(nk):
                        act(pt[:, :, m, :], sbt[:, :, m, :], AF.Exp, [bsb, b_flg], [bpt],
                            bias=flg_sb[:, 2 + slot + m:3 + slot + m])
                pts[j] = (pt, bpt)

            def PV(j):
                p_, rq = steps[j]
                t_ = tiles[p_]
                ws, nk, slot = NA_PLAN[rq]
                tti, par = divmod(rq, 2)
                bko, bbo = banks[4 + tti % 2], pb[4 + tti % 2]
                pt, bpt = pts.pop(j)
                for hh in range(2):
                    for m in range(nk):
                        if ws % 2 == 0:
                            vt, bv = t_["va"][:, ws // 2 + m, hh, :], t_["bva"]
                        else:
                            vt, bv = t_["vb"][:, (ws - 1) // 2 + m, hh, :], t_["bvb"]
                        mm(bko[par * 64:par * 64 + 64, hh * 65:(hh + 1) * 65], pt[:, hh, m, :], vt,
                           m == 0, m == nk - 1, [bpt, bv], [bbo])
                if par == 1:
                    rc, brc = rc_r.next()
                    o_v = bko[:, 0:130].rearrange("p (h d) -> p h d", d=65)
                    recip(rc, o_v[:, :, 64], [bbo], [brc])
                    ob, bob = ob_r.next()
                    tt("dve", ob.rearrange("p (h d) -> p h d", d=64), o_v[:, :, 0:64],
                       rc.unsqueeze(2).to_broadcast([128, 2, 64]), ALU.mult, [bbo, brc], [bob])
                    bk2, bb2 = banks[6 + tti % 2], pb[6 + tti % 2]
                    pbf = bk2[:, :].bitcast(BF16)
                    tr(pbf[:, 0:128], ob, ident_bf, [bob, b_idb], [bb2])
                    cp("act", t_["oT"][:, tti * 128:(tti + 1) * 128], pbf[:, 0:128], [bb2], [t_["boT"]])
                if rq == 63:
                    P.dma("sp", mixT[512 + p_ * 128:512 + (p_ + 1) * 128, :], t_["oT"], [t_["boT"]], [], t_["boT"])

            QK(0)
            QK(1)
            for j, (p_, rq) in enumerate(steps):
                if rq == 0 and p_ + 1 < 6:
                    tiles[p_ + 1] = load_pair(p_ + 1)
                EW(j)
                if j + 2 < len(steps):
                    QK(j + 2)
                PV(j)
            P.barrier()

        def stageB3(l):
            A.reset()
            pb = PB("B3ps")
            lam_init = 0.8 - 0.6 * math.exp(-0.3 * l)
            lq = A.alloc([256], F32)
            b_l = Buf("lam")
            P.dma("sp", lq, dlam[l], [], [b_l], b_l)
            lt = A.alloc([128], F32)
            lv = A.alloc([4], F32)
            tt("dve", lt[:, 0:64], lq[:, 0:64], lq[:, 64:128], ALU.mult, [b_l], [b_l])
            tt("dve", lt[:, 64:128], lq[:, 128:192], lq[:, 192:256], ALU.mult, [b_l], [b_l])
            rsum("dve", lv[:, 0:2], lt.rearrange("p (a b) -> p a b", b=64), [b_l], [b_l])
            act(lv[:, 2:4], lv[:, 0:2], AF.Exp, [b_l], [b_l])
            tt("dve", lv[:, 0:1], lv[:, 3:4], lv[:, 2:3], ALU.subtract, [b_l], [b_l])
            ts("dve", lv[:, 1:2], lv[:, 0:1], -lam_init, None, ALU.add, None, [b_l], [b_l])
            neglam = lv[:, 1:2]
            nw = A.alloc([768], F32)
            b_nw = Buf("dnw")
            P.dma("sp", nw, dnw[l], [], [b_nw], b_nw)
            ts("dve", nw, nw, 1.0 - lam_init, None, ALU.mult, None, [b_nw], [b_nw])
            cB = A.alloc([CW - C_IOTA], F32)
            b_cB = Buf("cB")
            P.dma("sp", cB, cst[:, C_IOTA:CW], [], [b_cB], b_cB)
            O_IOTA, O_NEAR, O_BCA, O_BCB = 0, C_NEAR - C_IOTA, C_BCA - C_IOTA, C_BCB - C_IOTA
            BCX = A.alloc([192], F32)
            BCBX = A.alloc([192], F32)
            b_bcx = Buf("bcx")
            ts("dve", BCX, cB[:, O_BCA:O_BCA + 192], xneg_col, None, ALU.add, None, [b_cB, b_flg], [b_bcx])
            ts("dve", BCBX, cB[:, O_BCB:O_BCB + 192], xneg_col, None, ALU.add, None, [b_cB, b_flg], [b_bcx])
            zero_col = A.alloc([1], F32)
            mset("dve", zero_col, 0.0, [b_bcx])
            q_r = Ring(A, 2, [2, T], BF16, "dq")
            k_r = Ring(A, 2, [2, T], BF16, "dk")
            for (kt_, kb_) in k_r.items:
                for c in range(2):
                    cp("pool" if c else "dve", kt_[64:128, c, :], cst_sb[64:128, C_KAUG:C_KAUG + 1].to_broadcast([64, T]), [b_cst], [kb_])
            vr_r = Ring(A, 2, [32, 128], BF16, "dvraw")
            v1_r = Ring(A, 2, [32, 129], BF16, "dv1")
            sb_r = Ring(A, 2, [2, 512], F32, "dsb")
            pt_r = Ring(A, 3, [2, 512], BF16, "dpt")
            sm_r = Ring(A, 2, [8], F32, "dsm")
            o_r = Ring(A, 2, [128], F32, "do")
            sq_r = Ring(A, 2, [128], F32, "dsq")
            ob_r = Ring(A, 2, [128], BF16, "dob")
            ot_r = Ring(A, 2, [T], BF16, "dot")
            for (v, b) in v1_r.items:
                mset("pool", v, 1.0, [b])
            iota2 = cB[:, O_IOTA:O_IOTA + 512].unsqueeze(1).to_broadcast([128, 2, 512])
            asb_r = Ring(A, 2, [3, 387], F32, "dasb")
            fz_r = Ring(A, 2, [4, 8], F32, "dfz")
            oa_r = Ring(A, 2, [4, 128], F32, "doa")
            sq4_r = Ring(A, 1, [4, 128], F32, "dsq4")
            ob8_r = Ring(A, 8, [128], BF16, "dob8")
            accs = []
            for c in range(2):
                for qs in range(4):
                    idx = c * 4 + qs
                    accs.append((banks[4 + idx // 3][:, (idx % 3) * 129:(idx % 3) * 129 + 129], pb[4 + idx // 3], idx))
            def load_head(h):
                qT, bq = q_r.next()
                kT, bk_ = k_r.next()
                for c in range(2):
                    r0 = h * 128 + c * 64
                    P.dma("sp", qT[0:64, c, :], qkT[2560 + r0:2560 + r0 + 64, :], [], [bq], bq)
                    P.dma("sp", kT[0:64, c, :], qkT[3328 + r0:3328 + r0 + 64, :], [], [bk_], bk_)
                    asrc = augq[h].unsqueeze(1).to_broadcast([4, 8, 512])
                    P.dma("pool", qT[64:68, c, :].rearrange("p (a b) -> p a b", b=512), asrc, [], [bq], bq)
                vr, bvr = vr_r.next()
                c0 = 2304 + h * 128
                P.dma("sp", vr, tokm[:, c0:c0 + 128].rearrange("(c p) f -> p c f", p=128), [], [bvr], bvr)
                v1, bv1 = v1_r.next()
                cp("pool", v1[:, :, 0:128], vr, [bvr], [bv1])
                oT, boT = ot_r.next()
                return (qT, bq, kT, bk_, v1, bv1, oT, boT)

            heads = {0: load_head(0)}
            for h in range(6):
                qT, bq, kT, bk_, v1, bv1, oT, boT = heads[h]
                if h + 1 < 6:
                    heads[h + 1] = load_head(h + 1)
                slope = SLOPES[h]
                pts = {}
                pend = []
                blocks = []
                for qb in range(8):
                    lst = []
                    for kt in range(32):
                        d = kt - 4 * qb
                        if d >= 4:
                            dmin = 128 * d - 511
                        elif d < 0:
                            dmin = 128 * (-d) - 127
                        else:
                            dmin = 0
                        if slope * dmin >= 200.0:
                            continue
                        lst.append(kt)
                    for kt in lst:
                        blocks.append((qb, kt, kt == lst[0], kt == lst[-1]))
                NB = len(blocks)

                def QK(j):
                    qb, kt, first, last = blocks[j]
                    b0 = 2 * (j % 2)
                    d = kt - 4 * qb
                    win = slice(0, 66) if d >= 4 else (slice(0, 68) if d < 0 else slice(0, 64))
                    for c in range(2):
                        mm(banks[b0 + c], kT[win, c, kt * 128:(kt + 1) * 128],
                           qT[win, c, qb * 512:(qb + 1) * 512], True, True, [bk_, bq], [pb[b0 + c]])

                def EW(j):
                    qb, kt, first, last = blocks[j]
                    b0 = 2 * (j % 2)
                    cross = (kt // 16) != (qb // 4)
                    d = kt - 4 * qb
                    rd = [b_cB, b_bcx]
                    pt, bpt = pt_r.next()
                    s2 = psum_all[:, b0 * 512:(b0 + 2) * 512].rearrange("p (c n) -> p c n", n=512)
                    if d >= 4:
                        col = (BCX if cross else cB[:, O_BCA:O_BCA + 192])[:, h * 32 + d:h * 32 + d + 1]
                        act(pt, s2, AF.Exp, [pb[b0], pb[b0 + 1]] + rd, [bpt], bias=col)
                    elif d < 0:
                        dd = -d
                        col = (BCBX if cross else cB[:, O_BCB:O_BCB + 192])[:, h * 32 + dd:h * 32 + dd + 1]
                        act(pt, s2, AF.Exp, [pb[b0], pb[b0 + 1]] + rd, [bpt], bias=col)
                    else:
                        sbt, bsb = sb_r.next()
                        in0 = cB[:, O_NEAR + d * 512:O_NEAR + (d + 1) * 512].unsqueeze(1).to_broadcast([128, 2, 512])
                        stt("dve", sbt, in0, slope, s2, ALU.mult, ALU.add, [pb[b0], pb[b0 + 1]] + rd, [bsb])
                        act(pt, sbt, AF.Exp, [bsb] + rd, [bpt])
                    pts[j] = (pt, bpt)

                def PV(j):
                    qb, kt, first, last = blocks[j]
                    pt, bpt = pts.pop(j)
                    for c in range(2):
                        for qs in range(4):
                            ap_, bb_, idx = accs[c * 4 + qs]
                            mm(ap_, pt[:, c, qs * 128:(qs + 1) * 128], v1[:, kt, :],
                               first and idx % 3 == 0, last, [bpt, bv1], [bb_])
                    if last:
                        finalize(qb, j)

                def finalize(qb, j):
                    asb, basb = asb_r.next()
                    for b in range(3):
                        cp("dve", asb[:, b, :], banks[4 + b][:, 0:387], [pb[4 + b]], [basb])
                    fz, bfz = fz_r.next()
                    oa, boa = oa_r.next()
                    for qs in range(4):
                        i0_, i1_ = qs, 4 + qs
                        a0_ = asb[:, i0_ // 3, (i0_ % 3) * 129:(i0_ % 3) * 129 + 129]
                        a1_ = asb[:, i1_ // 3, (i1_ % 3) * 129:(i1_ % 3) * 129 + 129]
                        recip(fz[:, qs, 0:1], a0_[:, 128:129], [basb], [bfz])
                        recip(fz[:, qs, 1:2], a1_[:, 128:129], [basb], [bfz])
                        tt("dve", fz[:, qs, 2:3], fz[:, qs, 1:2], neglam, ALU.mult, [bfz, b_l], [bfz])
                        ts("dve", oa[:, qs, :], a0_[:, 0:128], fz[:, qs, 0:1], None, ALU.mult, None, [basb, bfz], [boa])
                        stt("dve", oa[:, qs, :], a1_[:, 0:128], fz[:, qs, 2:3], oa[:, qs, :], ALU.mult, ALU.add,
                            [basb, bfz, boa], [boa])
                    sq4, bsq4 = sq4_r.next()
                    tt("dve", sq4, oa, oa, ALU.mult, [boa], [bsq4])
                    rsum("dve", fz[:, :, 3], sq4, [bsq4], [bfz])

                    def F2(fz=fz, bfz=bfz):
                        act(fz[:, :, 4], fz[:, :, 3], AF.Ln, [bfz, b_ones], [bfz], bias=eps_col, scale=1.0 / 128.0)
                        act(fz[:, :, 5], fz[:, :, 4], AF.Exp, [bfz], [bfz], scale=-0.5)

                    def F3(fz=fz, bfz=bfz, oa=oa, boa=boa, qb=qb):
                        for qs in range(4):
                            ob, bob = ob8_r.next()
                            stt("dve", ob, oa[:, qs, :], fz[:, qs, 5:6], nw[:, h * 128:(h + 1) * 128], ALU.mult, ALU.mult,
                                [boa, bfz, b_nw], [bob])
                            tti = qb * 4 + qs

                            def fin(ob=ob, bob=bob, tti=tti):
                                bk2, bb2 = banks[7], pb[7]
                                pbf = bk2[:, :].bitcast(BF16)
                                tr(pbf[:, (tti % 2) * 128:(tti % 2) * 128 + 128], ob, ident_bf, [bob, b_idb], [bb2])
                                cp("dve", oT[:, tti * 128:(tti + 1) * 128], pbf[:, (tti % 2) * 128:(tti % 2) * 128 + 128],
                                   [bb2], [boT])
                            pend.append((j + 10, fin))
                    pend.append((j + 3, F2))
                    pend.append((j + 6, F3))
                    pend.sort(key=lambda t_: t_[0])

                QK(0)
                QK(1)
                for j in range(NB):
                    EW(j)
                    if j + 2 < NB:
                        QK(j + 2)
                    PV(j)
                    pend.sort(key=lambda t_: t_[0])
                    while pend and pend[0][0] <= j:
                        pend.pop(0)[1]()
                        pend.sort(key=lambda t_: t_[0])
                while pend:
                    pend.pop(0)[1]()
                P.dma("sp", mixT[1280 + h * 128:1280 + (h + 1) * 128, :], oT, [boT], [], boT)
            P.barrier()

        def stageC(l):
            A.reset()
            pb = PB("Cps")
            wo = A.alloc([16, D], BF16)
            b_wo = [Buf("wo%d" % i) for i in range(4)]
            w_v = w_out[l].rearrange("(c p) n -> p c n", p=128)
            stg_r = Ring(A, 4, [2048], F32, "stgC")
            for q in range(16):
                load_cast(stg_r, "sp", "pool", wo[:, :, q * 128:(q + 1) * 128], b_wo[q // 4], w_v[:, :, q * 128:(q + 1) * 128])
            mx_r = Ring(A, 2, [16, 512], BF16, "mx")
            xs_r = Ring(A, 3, [512], F32, "xc")
            sq_r = Ring(A, 3, [512], BF16, "sqc")
            hb_r = Ring(A, 3, [512], BF16, "hbc")
            tmp_r = Ring(A, 2, [512], F32, "tmpC")
            rr_r = Ring(A, 2, [512], F32, "rrC")
            zc = A.alloc([16, 32], BF16)
            b_zc = Buf("zc")
            mset("dve", zc, 0.0, [b_zc])
            h2_v = h2T.rearrange("(c p) t -> p c t", p=128)
            P.dma("sp", h2_v[:, :, 0:32], zc, [b_zc], [], b_zc)
            P.dma("sp", h2_v[:, :, T + 32:T + 64], zc, [b_zc], [], b_zc)
            zc32 = zc.rearrange("p a b -> p (a b)")[:, 0:32].bitcast(F32)
            P.dma("sp", rs2[:, 0:16], zc32, [b_zc], [], b_zc)
            P.dma("sp", rs2[:, T + 16:T + 32], zc32, [b_zc], [], b_zc)
            P.barrier()
            w2 = nrm_sb[:, 2 + l, :]
            mix_v = mixT.rearrange("(c p) t -> p c t", p=128)
            k = 0
            pend_sq = None
            for blk in range(8):
                sl = slice(blk * 512, (blk + 1) * 512)
                mx, bmx = mx_r.next()
                P.dma("sp", mx, mix_v[:, :, sl], [], [bmx], bmx)
                bks, bbs = banks[6 + blk % 2], pb[6 + blk % 2]
                for oc in range(16):
                    xs, bx = xs_r.next()
                    P.dma("sp", xs, xT[oc * 128:(oc + 1) * 128, sl], [], [bx], bx)
                    bk, bb = banks[k % 4], pb[k % 4]
                    k += 1
                    for c in range(16):
                        mm(bk[:, :], wo[:, c, oc * 128:(oc + 1) * 128], mx[:, c, :], c == 0, c == 15,
                           [b_wo[oc // 4], bmx], [bb])
                    tt("dve", xs, xs, bk[:, :], ALU.add, [bx, bb], [bx])
                    P.dma("pool", xT[oc * 128:(oc + 1) * 128, sl], xs, [bx], [], bx)
                    sq, bs = sq_r.next()
                    act(sq, xs, AF.Square, [bx], [bs])
                    if pend_sq is not None:
                        pend_sq()

                    def _ssq(sq=sq, bs=bs, oc=oc, bks=bks, bbs=bbs):
                        mm(bks[:, :], ones128, sq, oc == 0, oc == 15, [bs, b_ones], [bbs])
                    pend_sq = _ssq
                    hb, bh = hb_r.next()
                    amul(hb, xs, w2[:, oc:oc + 1], [bx, b_nrm], [bh])
                    P.dma("pool", h2T[oc * 128:(oc + 1) * 128, 32 + blk * 512:32 + (blk + 1) * 512], hb, [bh], [], bh)
                pend_sq()
                pend_sq = None
                tm, bt = tmp_r.next()
                act(tm, bks[:, :], AF.Sqrt, [bbs, b_ones], [bt], bias=eps_col, scale=1.0 / D)
                rr, brr = rr_r.next()
                recip(rr, tm, [bt], [brr])
                P.dma("sp", rs2[:, 16 + blk * 512:16 + (blk + 1) * 512], rr, [brr], [], brr)
            P.barrier()

        def stageD(l):
            A.reset()
            pb = PB("Dps")
            TS = 1024
            cw = A.alloc([44, 4], F32)
            b_cw = Buf("cw")
            P.dma("sp", cw, cwb[l], [], [b_cw], b_cw)
            h2 = A.alloc([16, TS + 2], BF16)
            b_h2 = Buf("h2")
            actT = A.alloc([44, TS], BF16)
            b_act = [Buf("act%d" % i) for i in range(2)]
            rst = A.alloc([TS + 2], F32)
            b_rst = Buf("rst")
            wu_r = Ring(A, 2, [2, 16, 128], BF16, "wu")
            wd_r = Ring(A, 3, [22, 128], BF16, "wd")
            stg_r = Ring(A, 2, [2048], F32, "stgD")
            G_r = Ring(A, 1, [514], F32, "G")
            c_r = Ring(A, 1, [512], F32, "cc")
            ge_r = Ring(A, 1, [512], F32, "ge")
            xs_r = Ring(A, 1, [512], F32, "xd")
            h2_v = h2T.rearrange("(c p) t -> p c t", p=128)
            wu_v = w_up[l].rearrange("(c p) n -> p c n", p=128)
            wd_v = w_dn[l].rearrange("(c p) n -> p c n", p=128)
            k = 0
            def load_h2(t0):
                P.dma("sp", h2, h2_v[:, :, 31 + t0:31 + t0 + TS + 2], [], [b_h2], b_h2)
                P.dma("sp", rst, rs2[:, 15 + t0:15 + t0 + TS + 2], [], [b_rst], b_rst)
                if t0 + TS == 2048:
                    ts("dve", h2[:, :, TS + 1:TS + 2], h2[:, :, TS + 1:TS + 2], keep_col, None, ALU.mult, None,
                       [b_h2, b_flg], [b_h2])
                if t0 == 2048:
                    ts("dve", h2[:, :, 0:1], h2[:, :, 0:1], keep_col, None, ALU.mult, None, [b_h2, b_flg], [b_h2])

            load_h2(0)
            for sbk in range(T // TS):
                t0 = sbk * TS
                def up_load(fc):
                    wu, bwu = wu_r.next()
                    load_cast(stg_r, "sp", "act", wu[:, 0, :, :], bwu, wu_v[:, :, fc * 128:(fc + 1) * 128])
                    load_cast(stg_r, "sp", "act", wu[:, 1, :, :], bwu, wu_v[:, :, 5632 + fc * 128:5632 + (fc + 1) * 128])
                    return (wu, bwu)

                def dn_load(step):
                    oc, hf = divmod(step, 2)
                    wdh, bwdh = wd_r.next()
                    for pc in range(2):
                        load_cast(stg_r, "sp", "act", wdh[:, pc * 11:(pc + 1) * 11, :], bwdh,
                                  wd_v[:, hf * 22 + pc * 11:hf * 22 + (pc + 1) * 11, oc * 128:(oc + 1) * 128])
                    return (wdh, bwdh)

                nxt = up_load(0)
                for fc in range(44):
                    wu, bwu = nxt
                    if fc + 1 < 44:
                        nxt = up_load(fc + 1)
                    else:
                        dn_q = [dn_load(0), dn_load(1)]
                    for sub in range(TS // 512):
                        c0 = 1 + sub * 512
                        bka, bba = banks[(2 * k) % 4], pb[(2 * k) % 4]
                        bkg, bbg = banks[(2 * k + 1) % 4], pb[(2 * k + 1) % 4]
                        bkh, bbh = banks[4 + k % 2], pb[4 + k % 2]
                        k += 1
                        for c in range(16):
                            mm(bka[:, :], wu[:, 0, c, :], h2[:, c, c0:c0 + 512], c == 0, c == 15, [bwu, b_h2], [bba])
                        for c in range(16):
                            mm(bkg[:, :], wu[:, 1, c, :], h2[:, c, c0:c0 + 512], c == 0, c == 15, [bwu, b_h2], [bbg])
                        for c in range(16):
                            mm(bkh[:, 0:2], wu[:, 1, c, :], h2[:, c, c0 - 1:c0 + 513:513], c == 0, c == 15, [bwu, b_h2], [bbh])
                        G, bG = G_r.next()
                        tt("dve", G[:, 1:513], bkg[:, :], rst[:, c0:c0 + 512], ALU.mult, [bbg, b_rst], [bG])
                        tt("dve", G[:, 0:514:513], bkh[:, 0:2], rst[:, c0 - 1:c0 + 513:513], ALU.mult, [bbh, b_rst, bG], [bG])
                        cc, bc = c_r.next()
                        ts("dve", cc, G[:, 1:513], cw[:, fc, 1:2], cw[:, fc, 3:4], ALU.mult, ALU.add, [bG, b_cw], [bc])
                        stt("dve", cc, G[:, 0:512], cw[:, fc, 0:1], cc, ALU.mult, ALU.add, [bG, b_cw, bc], [bc])
                        stt("dve", cc, G[:, 2:514], cw[:, fc, 2:3], cc, ALU.mult, ALU.add, [bG, b_cw, bc], [bc])
                        ge, bge = ge_r.next()
                        act(ge, cc, AF.Gelu_apprx_tanh, [bc], [bge])
                        tt("pool", ge, ge, rst[:, c0:c0 + 512], ALU.mult, [bge, b_rst], [bge])
                        tt("dve", actT[:, fc, sub * 512:(sub + 1) * 512], bka[:, :], ge, ALU.mult, [bba, bge], [b_act[sub]])
                if t0 + TS < T:
                    load_h2(t0 + TS)
                for oc in range(16):
                    wdA, bwdA = dn_q.pop(0)
                    wdB, bwdB = dn_q.pop(0)
                    if oc + 1 < 16:
                        dn_q.append(dn_load(2 * oc + 2))
                    for sub in range(TS // 512):
                        sl = slice(t0 + sub * 512, t0 + (sub + 1) * 512)
                        xs, bx = xs_r.next()
                        P.dma("sp", xs, xT[oc * 128:(oc + 1) * 128, sl], [], [bx], bx)
                        bk, bb = banks[6 + k % 2], pb[6 + k % 2]
                        k += 1
                        for fc in range(44):
                            if fc == 22 and sub == TS // 512 - 1 and oc + 1 < 16:
                                dn_q.append(dn_load(2 * oc + 3))
                            wdx, bwdx = (wdA, bwdA) if fc < 22 else (wdB, bwdB)
                            mm(bk[:, :], wdx[:, fc % 22, :], actT[:, fc, sub * 512:(sub + 1) * 512], fc == 0, fc == 43,
                               [bwdx, b_act[sub]], [bb])
                        tt("dve", xs, xs, bk[:, :], ALU.add, [bx, bb], [bx])
                        P.dma("pool", xT[oc * 128:(oc + 1) * 128, sl], xs, [bx], [], bx)
            P.barrier()

        def stageF():
            A.reset()
            pb = PB("Fps")
            xs_all_r = Ring(A, 2, [16, 512], F32, "xf")
            sq_r = Ring(A, 3, [512], BF16, "sqf")
            tmp_r = Ring(A, 2, [512], F32, "tmpF")
            rr_r = Ring(A, 2, [512], F32, "rrF")
            yo_r = Ring(A, 2, [D], F32, "yo")
            wf = nrm_sb[:, 4, :]
            xT_v = xT.rearrange("(c p) t -> p c t", p=128)
            k = 0
            for blk in range(8):
                sl = slice(blk * 512, (blk + 1) * 512)
                xa, bxa = xs_all_r.next()
                P.dma("sp", xa, xT_v[:, :, sl], [], [bxa], bxa)
                bks, bbs = banks[6 + blk % 2], pb[6 + blk % 2]
                for c in range(16):
                    sq, bs = sq_r.next()
                    act(sq, xa[:, c, :], AF.Square, [bxa], [bs])
                    mm(bks[:, :], ones128, sq, c == 0, c == 15, [bs, b_ones], [bbs])
                tm, bt = tmp_r.next()
                act(tm, bks[:, :], AF.Sqrt, [bbs, b_ones], [bt], bias=eps_col, scale=1.0 / D)
                rr, brr = rr_r.next()
                recip(rr, tm, [bt], [brr])
                for c in range(16):
                    stt("dve", xa[:, c, :], xa[:, c, :], wf[:, c:c + 1], rr, ALU.mult, ALU.mult,
                        [bxa, b_nrm, brr], [bxa])
                for j in range(4):
                    tti = blk * 4 + j
                    yo, byo = yo_r.next()
                    for g in range(4):
                        bk, bb = banks[k % 4], pb[k % 4]
                        k += 1
                        for jj in range(4):
                            c = g * 4 + jj
                            tr(bk[:, jj * 128:(jj + 1) * 128], xa[:, c, j * 128:(j + 1) * 128], ident, [bxa, b_cst], [bb])
                        cp("act" if g % 2 else "dve", yo[:, g * 512:(g + 1) * 512], bk[:, :], [bb], [byo])
                    final_ops.append(P.dma("sp", y_out[tti * 128:(tti + 1) * 128, :], yo, [byo], [], byo))
            P.barrier()

        P.barrier()

        if "s0" in stages:
            stage0()
        for l in range(nlayers):
            if "A" in stages:
                stageA(l)
            if "B1" in stages:
                stageB1(l)
            if "B2" in stages:
                stageB2(l)
            if "B3" in stages:
                stageB3(l)
            if "C" in stages:
                stageC(l)
            if "D" in stages:
                stageD(l)
        if "F" in stages:
            stageF()
        P.emit(final_ops)
    return nc


def prep_inputs(inp):
    f = lambda a: np.ascontiguousarray(np.asarray(a, dtype=np.float32))
    xp, xs = f(inp["x_prompt"]), f(inp["x_sample"])
    shared = {
        "w_in": f(inp["w_in"]), "w_out": f(inp["w_out"]), "w_up": f(inp["ffn_w_up"]), "w_dn": f(inp["ffn_w_down"]),
    }
    n1, n2, nf = f(inp["norm1_w"]), f(inp["norm2_w"]), f(inp["final_norm_w"])
    nrm = np.stack([n1[0], n1[1], n2[0], n2[1], nf], 0).reshape(5, 16, 128).transpose(2, 0, 1)
    shared["nrm"] = np.ascontiguousarray(nrm)
    cw, cb = f(inp["ffn_conv_w"]), f(inp["ffn_conv_b"])
    cwb = np.concatenate([cw, cb[:, None, :]], 1)
    shared["cwb"] = np.ascontiguousarray(cwb.reshape(2, 4, 44, 128).transpose(0, 3, 2, 1))
    df, db = f(inp["ret_decay_fwd"]), f(inp["ret_decay_bwd"])
    rdec = np.zeros((2, 128, 24), np.float32)
    hp = np.arange(128) // 64
    for l in range(2):
        rdec[l, :, 0:8] = df[l][None, :]
        rdec[l, :, 8:16] = db[l][None, :]
        for pr in range(4):
            rdec[l, :, 16 + pr] = df[l][2 * pr + hp]
            rdec[l, :, 20 + pr] = db[l][2 * pr + hp]
    shared["rdec"] = rdec
    shared["rnw"] = np.ascontiguousarray(np.broadcast_to(f(inp["ret_norm_w"])[:, None, :], (2, 128, 512)))
    shared["dnw"] = np.ascontiguousarray(np.broadcast_to(f(inp["diff_norm_w"])[:, None, :], (2, 128, 768)))
    dl = np.concatenate([f(inp["diff_lambda_q1"]), f(inp["diff_lambda_k1"]), f(inp["diff_lambda_q2"]),
                         f(inp["diff_lambda_k2"])], 1)
    shared["dlam"] = np.ascontiguousarray(np.broadcast_to(dl[:, None, :], (2, 128, 256)))
    shared["tt"] = make_tt(f(inp["na_rpb"]))
    shared["cst"] = make_consts()
    import ml_dtypes
    aug = np.zeros((6, 4, 512), np.float32)
    ii = np.arange(512, dtype=np.float64)
    for h in range(6):
        v = (SLOPES[h] * ii).astype(np.float32)
        hi = v.astype(ml_dtypes.bfloat16).astype(np.float32)
        lo = (v - hi).astype(ml_dtypes.bfloat16).astype(np.float32)
        aug[h, 0], aug[h, 1], aug[h, 2], aug[h, 3] = hi, lo, hi, lo
    shared["augq"] = aug
    in_maps = []
    for c in range(8):
        m = dict(shared)
        if c < 4:
            m["x"] = np.ascontiguousarray(xp[2 * c:2 * c + 2].reshape(T, D))
            m["flg"] = make_flags(0)
        else:
            m["x"] = np.ascontiguousarray(xs[c - 4].reshape(T, D))
            m["flg"] = make_flags(1)
        in_maps.append(m)
    return in_maps


_NC = None


def kernel(**inputs):
    global _NC
    in_maps = prep_inputs(inputs)
    if _NC is None:
        _NC = build()
    res = run_bass_kernel_spmd(_NC, in_maps, core_ids=list(range(8)))
    ys = [np.asarray(r["y"], dtype=np.float32) for r in res.results]
    y_prompt = np.stack([ys[c].reshape(2, 2048, D) for c in range(4)], 0).reshape(8, 2048, D)
    y_sample = np.stack([ys[c].reshape(4096, D) for c in range(4, 8)], 0)
    return (y_prompt, y_sample)
```
